# Optimizing a Trainium2 kernel written in Bass

```python
import math
import jax, jax.numpy as jnp
from jax import lax
import numpy as np

D_MODEL = 2048
BATCH = 4
SEQ = 2048
DEPTH = 1

MIX_WIDTH = D_MODEL
ATTN_HEADS = 8
HEAD_DIM = 128
ATTN_WIDTH = ATTN_HEADS * HEAD_DIM
LRU_HEADS = 8
LRU_WIDTH = MIX_WIDTH - ATTN_WIDTH
LRU_BLOCK = LRU_WIDTH // LRU_HEADS
CONV_WIDTH = 4
LRU_C = 8.0
D_FF = 4 * D_MODEL
Q_BLOCK = 128
N_MOD = 6
IN_COLS = 3 * ATTN_WIDTH + 2 * LRU_WIDTH
EPS = 1e-6

kernel_name = "hymba_stickbreak_rglru_sqrelu_adaln"


def rmsnorm(x, g):
    x32 = x.astype(jnp.float32)
    y = x32 * lax.rsqrt(jnp.mean(x32 * x32, axis=-1, keepdims=True) + EPS)
    return (y * g.astype(jnp.float32)).astype(x.dtype)


def stick_breaking_attention(q, k, v):
    B, S, H, Dh = q.shape
    scale = Dh ** -0.5
    outs = []
    for i in range(S // Q_BLOCK):
        q0 = i * Q_BLOCK
        kend = q0 + Q_BLOCK
        qb = q[:, q0:kend].astype(jnp.float32)
        kb = k[:, :kend].astype(jnp.float32)
        vb = v[:, :kend].astype(jnp.float32)
        z = jnp.einsum('bqhd,bkhd->bhqk', qb, kb) * scale
        t_idx = q0 + jnp.arange(Q_BLOCK)[:, None]
        s_idx = jnp.arange(kend)[None, :]
        causal = s_idx < t_idx
        log_beta = jax.nn.log_sigmoid(z)
        log_stay = jnp.where(causal, jax.nn.log_sigmoid(-z), 0.0)
        rev = lax.cumsum(log_stay, axis=3, reverse=True)
        after = jnp.concatenate([rev[..., 1:], jnp.zeros_like(rev[..., :1])], axis=-1)
        w = jnp.where(causal, jnp.exp(log_beta + after), 0.0)
        outs.append(jnp.einsum('bhqk,bkhd->bqhd', w, vb))
    return jnp.concatenate(outs, axis=1).astype(q.dtype)


def causal_depthwise_conv(x, w, b):
    C = x.shape[-1]
    y = lax.conv_general_dilated(
        x.astype(jnp.float32), w.astype(jnp.float32)[:, None, :],
        window_strides=(1,), padding=[(CONV_WIDTH - 1, 0)],
        dimension_numbers=('NWC', 'WIO', 'NWC'), feature_group_count=C)
    return y + b.astype(jnp.float32)


def rg_lru(x, w_a, b_a, w_x, b_x, lam):
    B, S, _ = x.shape
    xh = x.reshape(B, S, LRU_HEADS, LRU_BLOCK)
    r = jax.nn.sigmoid(jnp.einsum('bshc,hcd->bshd', xh, w_a.astype(jnp.float32)).reshape(B, S, LRU_WIDTH) + b_a)
    i = jax.nn.sigmoid(jnp.einsum('bshc,hcd->bshd', xh, w_x.astype(jnp.float32)).reshape(B, S, LRU_WIDTH) + b_x)
    log_a = -LRU_C * r * jax.nn.softplus(-lam.astype(jnp.float32))
    a = jnp.exp(log_a)
    u = jnp.sqrt(-jnp.expm1(2.0 * log_a)) * (i * x)

    def combine(left, right):
        a1, b1 = left
        a2, b2 = right
        return a1 * a2, a2 * b1 + b2

    _, h = lax.associative_scan(combine, (a, u), axis=1)
    return h


def setup_inputs(seed: int = 0) -> dict:
    key = jax.random.key(seed)
    ks = jax.random.split(key, 24)
    f32 = jnp.float32
    nrm = lambda k, shape, s: jax.random.normal(k, shape, f32) * s
    u = jax.random.uniform(ks[11], (DEPTH, LRU_WIDTH), f32, 0.9, 0.999)
    a_base = u ** (1.0 / LRU_C)
    lru_lambda = jnp.log(a_base) - jnp.log1p(-a_base)
    return {
        "x": nrm(ks[0], (BATCH, SEQ, D_MODEL), 1.0),
        "c": nrm(ks[1], (BATCH, D_MODEL), 1.0),
        "w_ada": nrm(ks[2], (DEPTH, D_MODEL, N_MOD * D_MODEL), 0.5 * D_MODEL ** -0.5),
        "b_ada": nrm(ks[3], (DEPTH, N_MOD * D_MODEL), 0.02),
        "g_norm_mix": 1.0 + nrm(ks[4], (DEPTH, D_MODEL), 0.02),
        "w_in": nrm(ks[5], (DEPTH, D_MODEL, IN_COLS), D_MODEL ** -0.5),
        "w_conv": nrm(ks[6], (DEPTH, CONV_WIDTH, LRU_WIDTH), CONV_WIDTH ** -0.5),
        "b_conv": nrm(ks[7], (DEPTH, LRU_WIDTH), 0.02),
        "w_rg_a": nrm(ks[8], (DEPTH, LRU_HEADS, LRU_BLOCK, LRU_BLOCK), LRU_BLOCK ** -0.5),
        "b_rg_a": nrm(ks[9], (DEPTH, LRU_WIDTH), 0.02),
        "w_rg_x": nrm(ks[10], (DEPTH, LRU_HEADS, LRU_BLOCK, LRU_BLOCK), LRU_BLOCK ** -0.5),
        "b_rg_x": nrm(ks[12], (DEPTH, LRU_WIDTH), 0.02),
        "lru_lambda": lru_lambda,
        "g_attn_out": 1.0 + nrm(ks[13], (DEPTH, ATTN_WIDTH), 0.02),
        "g_lru_out": 1.0 + nrm(ks[14], (DEPTH, LRU_WIDTH), 0.02),
        "w_out": nrm(ks[15], (DEPTH, MIX_WIDTH, D_MODEL), MIX_WIDTH ** -0.5),
        "g_norm_mlp": 1.0 + nrm(ks[16], (DEPTH, D_MODEL), 0.02),
        "w_mlp_in": nrm(ks[17], (DEPTH, D_MODEL, D_FF), D_MODEL ** -0.5),
        "w_mlp_out": nrm(ks[18], (DEPTH, D_FF, D_MODEL), D_FF ** -0.5),
        "g_norm_final": 1.0 + nrm(ks[19], (D_MODEL,), 0.02),
    }


def reference(x, c, w_ada, b_ada, g_norm_mix, w_in, w_conv, b_conv, w_rg_a, b_rg_a,
              w_rg_x, b_rg_x, lru_lambda, g_attn_out, g_lru_out, w_out, g_norm_mlp,
              w_mlp_in, w_mlp_out, g_norm_final):
    B, S, D = x.shape
    c_act = jax.nn.silu(c.astype(jnp.float32))
    for l in range(DEPTH):
        mod = (c_act @ w_ada[l].astype(jnp.float32) + b_ada[l]).reshape(B, N_MOD, D)
        sh1, sc1, gt1, sh2, sc2, gt2 = [mod[:, j][:, None, :] for j in range(N_MOD)]

        h = rmsnorm(x, g_norm_mix[l]).astype(jnp.float32) * (1.0 + sc1) + sh1
        proj = jnp.einsum('bsd,de->bse', h, w_in[l].astype(jnp.float32))
        q, k, v, xr, xg = jnp.split(
            proj, np.cumsum([ATTN_WIDTH, ATTN_WIDTH, ATTN_WIDTH, LRU_WIDTH]).tolist(), axis=-1)

        to_heads = lambda t: t.reshape(B, S, ATTN_HEADS, HEAD_DIM)
        o_attn = stick_breaking_attention(to_heads(q), to_heads(k), to_heads(v)).reshape(B, S, ATTN_WIDTH)

        xr = causal_depthwise_conv(xr, w_conv[l], b_conv[l])
        hr = rg_lru(xr, w_rg_a[l], b_rg_a[l], w_rg_x[l], b_rg_x[l], lru_lambda[l])
        o_lru = hr * jax.nn.gelu(xg, approximate=True)

        mixed = jnp.concatenate([rmsnorm(o_attn, g_attn_out[l]), rmsnorm(o_lru, g_lru_out[l])], axis=-1)
        y = jnp.einsum('bse,ed->bsd', mixed, w_out[l].astype(jnp.float32))
        x = (x.astype(jnp.float32) + gt1 * y).astype(x.dtype)

        h = rmsnorm(x, g_norm_mlp[l]).astype(jnp.float32) * (1.0 + sc2) + sh2
        hid = jnp.square(jax.nn.relu(jnp.einsum('bsd,df->bsf', h, w_mlp_in[l].astype(jnp.float32))))
        y = jnp.einsum('bsf,fd->bsd', hid, w_mlp_out[l].astype(jnp.float32))
        x = (x.astype(jnp.float32) + gt2 * y).astype(x.dtype)

    return rmsnorm(x, g_norm_final)
```

```python
import contextlib
import numpy as np
import ml_dtypes
import concourse.bass as bass
import concourse.mybir as mybir
from concourse.bass_utils import run_bass_kernel_spmd

F32 = mybir.dt.float32
BF16 = mybir.dt.bfloat16
FP16 = mybir.dt.float16
AF = mybir.ActivationFunctionType
ALU = mybir.AluOpType

D = 2048
S = 2048
NH = 8
DFF = 8192
EPS = 1e-6
SCALE = 128.0 ** -0.5
LRU_C = 8.0
SEM_CAP = 30000
HILO = True

GMIX, GMLP, CT, WC, BCONV, BRGA, BRGX, LAM, GATT, GLRU, SEL, NV = 0, 16, 32, 48, 80, 88, 96, 104, 112, 120, 128, 130


class Op:
    __slots__ = ("eng", "emit", "deps", "signal", "sig_no", "idx", "dma_sem", "dma_val", "pre")

    def __init__(self, eng, emit):
        self.eng = eng
        self.emit = emit
        self.deps = []
        self.signal = False
        self.sig_no = None
        self.idx = None
        self.dma_sem = None
        self.dma_val = None
        self.pre = None


class Prog:
    ENGS = ("pe", "act", "dve", "pool", "sp")

    def __init__(self, nc, n_dma_sems=32):
        self.nc = nc
        self.ops = {e: [] for e in self.ENGS}
        self.last_writer = {}
        self.readers = {}
        self.n_dma_sems = n_dma_sems
        self.dma_rr = {"sp": 0, "pool": 0, "act": 0}
        self.dma_counts = [0] * n_dma_sems
        self.dma_last = [None] * n_dma_sems
        self.all_ops = []
        self.pending_dma = []
        self.last_op = {}
        self.bank_last = {}

    def _add_dep(self, sel, d):
        if d.dma_sem is not None:
            sel[("dma", id(d))] = d
        else:
            cur = sel.get(d.eng)
            if cur is None or cur.idx < d.idx:
                sel[d.eng] = d

    def _track(self, op, reads, writes):
        sel = {}
        for r in reads:
            w = self.last_writer.get(r)
            if w is not None:
                self._add_dep(sel, w)
        for k in writes:
            w = self.last_writer.get(k)
            if w is not None:
                self._add_dep(sel, w)
            for rd in self.readers.get(k, {}).values():
                self._add_dep(sel, rd)
        for r in reads:
            rr = self.readers.setdefault(r, {})
            key = op.eng if op.dma_sem is None else ("dma", id(op))
            rr[key] = op
        for k in writes:
            self.last_writer[k] = op
            self.readers[k] = {}
        op.deps = [d for d in sel.values() if d is not op]

    def _new(self, eng, emit):
        o = Op(eng, emit)
        o.idx = len(self.ops[eng])
        self.ops[eng].append(o)
        self.all_ops.append(o)
        self.last_op[eng] = o
        return o

    def op(self, eng, emit, reads=(), writes=(), banks=()):
        o = self._new(eng, emit)
        self._track(o, reads, writes)
        for b in banks:
            last = self.bank_last.get(b)
            if last is not None and last.eng != eng:
                cur = [d for d in o.deps if d.eng == last.eng and d.dma_sem is None]
                if not cur:
                    o.deps.append(last)
                elif cur[0].idx < last.idx:
                    o.deps.remove(cur[0])
                    o.deps.append(last)
            self.bank_last[b] = o
        return o

    def dma(self, queue, out_ap, in_ap, reads=(), writes=()):
        half = self.n_dma_sems // 2
        base = 0 if queue == "sp" else half
        s = base + self.dma_rr[queue]
        self.dma_rr[queue] = (self.dma_rr[queue] + 1) % half
        o = self._new(queue, lambda e, a=out_ap, b=in_ap: e.dma_start(out=a, in_=b))
        o.pre = self.dma_last[s]
        self.dma_counts[s] += 1
        o.dma_sem = s
        o.dma_val = 16 * self.dma_counts[s]
        self.dma_last[s] = o
        self.pending_dma.append(o)
        self._track(o, reads, writes)
        return o

    def barrier(self):
        deps = [o for e, o in self.last_op.items() if o.dma_sem is None and o.emit is not None]
        deps += self.pending_dma
        self.pending_dma = []
        for e in self.ENGS:
            o = self._new(e, None)
            o.deps = [d for d in deps]
        self.last_writer = {}
        self.readers = {}
        self.bank_last = {}

    def emit_all(self, final_waits=()):
        nc = self.nc
        for o in self.all_ops:
            for d in o.deps:
                if d.dma_sem is None:
                    d.signal = True
        n_sig = {}
        for e in self.ENGS:
            c = 0
            for o in self.ops[e]:
                if o.dma_sem is None and o.signal:
                    assert o.emit is not None
                    c += 1
                    o.sig_no = c
            n_sig[e] = c
        with contextlib.ExitStack() as st:
            eng_sems = {}
            for e in self.ENGS:
                k = max(1, (n_sig[e] + SEM_CAP - 1) // SEM_CAP)
                eng_sems[e] = [st.enter_context(nc.semaphore(f"s_{e}_{i}")) for i in range(k)]
            dma_sems = [st.enter_context(nc.semaphore(f"s_dma_{i}")) for i in range(self.n_dma_sems)]
            block = st.enter_context(nc.Block())

            def waits_for(deps, state):
                best = {}
                for d in deps:
                    if d.dma_sem is not None:
                        key = ("dma", d.dma_sem)
                        sem = dma_sems[d.dma_sem]
                        val = d.dma_val
                    else:
                        si = (d.sig_no - 1) // SEM_CAP
                        key = (d.eng, si)
                        sem = eng_sems[d.eng][si]
                        val = (d.sig_no - 1) % SEM_CAP + 1
                    if state.get(key, 0) >= val:
                        continue
                    state[key] = val
                    best[key] = (sem, val)
                return list(best.values())

            def run_engine(ename, handle):
                state = {}
                for o in self.ops[ename]:
                    deps = list(o.deps)
                    if o.pre is not None:
                        deps.append(o.pre)
                    for sem, val in waits_for(deps, state):
                        handle.wait_ge(sem, val)
                    if o.emit is None:
                        continue
                    r = o.emit(handle)
                    last = r[-1] if isinstance(r, (list, tuple)) else r
                    if o.dma_sem is not None:
                        last.then_inc(dma_sems[o.dma_sem], 16)
                    elif o.signal:
                        si = (o.sig_no - 1) // SEM_CAP
                        last.then_inc(eng_sems[ename][si], 1)
                if ename == "sp":
                    for sem, val in waits_for(list(final_waits), state):
                        handle.wait_ge(sem, val)

            block.tensor(lambda h: run_engine("pe", h))
            block.scalar(lambda h: run_engine("act", h))
            block.vector(lambda h: run_engine("dve", h))
            block.gpsimd(lambda h: run_engine("pool", h))
            block.sync(lambda h: run_engine("sp", h))


def build(debug=False, stop=None):
    nc = bass.Bass("TRN2", target_bir_lowering=False)
    dram = lambda n, s, dt=F32, kind="ExternalInput": nc.dram_tensor(n, s, dt, kind=kind).ap()
    x_all = dram("x_all", [S, D])
    x_own = dram("x_own", [S // 2, D])
    vecs_d = dram("vecs", [128, NV])
    b_ada = dram("b_ada", [1, 6 * D])
    masks_d = dram("masks", [128, 256])
    gfin_d = dram("gfin", [128, D])
    cbf_d = dram("cbf", [128, 640], BF16)
    w_ada = dram("w_ada", [D, 6 * D])
    w_in = dram("w_in", [D, 5120])
    w_rga = dram("w_rga", [NH, 128, 128])
    w_rgx = dram("w_rgx", [NH, 128, 128])
    w_out = dram("w_out", [D, D])
    w1 = dram("w1", [D, DFF])
    w2 = dram("w2", [DFF, D])
    out_d = dram("out", [S // 2, D], F32, "ExternalOutput")
    dbg = {}
    if debug:
        dbg["mods"] = dram("d_mods", [128, 64], F32, "ExternalOutput")
        dbg["hown"] = dram("d_hown", [128, 8192], F32, "ExternalOutput")
        dbg["kv"] = dram("d_kv", [128, 16384], F32, "ExternalOutput")
        dbg["r2"] = dram("d_r2", [128, 12288], F32, "ExternalOutput")
        dbg["st"] = dram("d_st", [128, 32], F32, "ExternalOutput")
        dbg["x1"] = dram("d_x1", [128, 16384], F32, "ExternalOutput")

    with contextlib.ExitStack() as st:
        sb = lambda n, s, dt=F32: st.enter_context(nc.sbuf_tensor("sb_" + n, s, dt))
        R1 = sb("R1", [128, 16384])
        R2 = sb("R2", [128, 12288])
        R3 = sb("R3", [128, 10880])
        WB = [sb(f"WB{i}", [128, 8192], BF16) for i in range(2)]
        vecs = sb("vecs", [128, NV])
        cbf = sb("cbf", [128, 640], BF16)
        masks = sb("masks", [128, 256])
        masks_bf = sb("masks_bf", [128, 256], BF16)
        gtbc = sb("gtbc", [128, D])
        mods = sb("mods", [128, 96])
        wrg = sb("wrg", [128, 2 * NH * 128], BF16)
        small = sb("small", [128, 96])
        cact = sb("cact", [128, 16], BF16)
        onef = sb("onef", [128, 128])
        brow = sb("brow", [1, 512])
        big = [st.enter_context(nc.psum_tensor(f"big{i}", [128, 1024], F32)) for i in range(3)]
        pbs = st.enter_context(nc.psum_tensor("pbs", [128, 512], F32))
        pbx = st.enter_context(nc.psum_tensor("pbx", [128, 512], F32))
        PBS, PBX = 6, 7
        pstv = [big[2][:, 0:512].bitcast(BF16), big[2][:, 512:1024].bitcast(BF16)]

        ident = cbf[:, 0:128]
        tri = cbf[:, 128:256]
        ones_bf = cbf[:, 256:384]
        ntri = cbf[:, 384:512]
        nones = cbf[:, 512:640]
        kT = R1[:, 0:8192].bitcast(BF16).rearrange("p (h t) -> p h t", h=NH)
        Vv = R1[:, 8192:16384].bitcast(BF16).rearrange("p (a c) -> p a c", a=16)
        hown = R1[:, 0:8192].rearrange("p (h t) -> p h t", h=NH)
        x1 = R1[:, :].rearrange("p (a d) -> p a d", a=8)
        qT = R2[:, 0:4096].bitcast(BF16).rearrange("p (h t) -> p h t", h=NH)
        mixA = R2[:, 4096:8192].bitcast(BF16).rearrange("p (h t) -> p h t", h=NH)
        mixL = R2[:, 8192:12288].bitcast(BF16).rearrange("p (h t) -> p h t", h=NH)
        hTg = R2[:, 4096:8192].bitcast(BF16).rearrange("p (k t) -> p k t", k=16)
        h2T = R2[:, 0:8192].bitcast(BF16).rearrange("p (k t) -> p k t", k=16)
        hid = R2[:, 8192:10240].bitcast(BF16).rearrange("p (f t) -> p f t", f=4)
        WBk = [w[:, :].rearrange("p (k c) -> p k c", k=16) for w in WB]
        WBf = [w[:, :].rearrange("p (f c) -> p f c", f=4) for w in WB]
        xst = [R3[:, 0:2048], R3[:, 2048:4096]]
        xn = [R3[:, 4096:5120].bitcast(BF16), R3[:, 5120:6144].bitcast(BF16)]
        L0 = 6144

        def r3(off, n, dt=F32):
            a = R3[:, off:off + n]
            return a if dt == F32 else a.bitcast(BF16)

        P = Prog(nc)

        def rsqrt_inplace(ap, key):
            P.op("act", lambda e: e.activation(ap, ap, AF.Ln), reads=[key], writes=[key])
            P.op("act", lambda e: e.activation(ap, ap, AF.Exp, scale=-0.5), reads=[key], writes=[key])
        V = lambda c0, n=1: vecs[:, c0:c0 + n]

        P.dma("sp", vecs[:], vecs_d, writes=["vecs"])
        P.dma("sp", cbf[:], cbf_d, writes=["cbf"])
        P.dma("sp", masks[:], masks_d, writes=["masks"])
        P.dma("pool", wrg[:, 0:1024].rearrange("p (h d) -> p h d", h=NH), w_rga.rearrange("h c d -> c h d"), writes=["wrga"])
        P.dma("pool", wrg[:, 1024:2048].rearrange("p (h d) -> p h d", h=NH), w_rgx.rearrange("h c d -> c h d"), writes=["wrgx"])
        P.op("dve", lambda e: e.memset(onef[:], 1.0), writes=["onef"])
        P.op("dve", lambda e: e.tensor_copy(masks_bf[:], masks[:]), reads=["masks"], writes=["masks_bf"])
        P.op("act", lambda e: e.activation(cact[:], V(CT, 16), AF.Silu), reads=["vecs"], writes=["cact"])
        cl = small[:, 0:8]
        P.op("act", lambda e: e.activation(small[:, 8:16], V(LAM, 8), AF.Exp, scale=-1.0), reads=["vecs"], writes=["cl_t"])
        P.op("act", lambda e: e.activation(small[:, 8:16], small[:, 8:16], AF.Ln, bias=1.0), reads=["cl_t"], writes=["cl_t"])
        P.op("dve", lambda e: e.tensor_scalar(cl, small[:, 8:16], -LRU_C, None, ALU.mult), reads=["cl_t"], writes=["cl"])
        carry = small[:, 16:24]
        P.op("dve", lambda e: e.memset(carry, 0.0), writes=["carry"])
        halo = small[:, 24:48].rearrange("p (h c) -> p h c", h=NH)
        P.op("dve", lambda e: e.memset(small[:, 24:48], 0.0), writes=["halo"])
        ssq = small[:, 48:56]
        rstd_a = small[:, 56:64]
        rstd_l = small[:, 64:72]
        stat_a = small[:, 72:80]
        rs_t = small[:, 80:88]

        wstate = {"n": 0}

        def wload_kc(src2d):
            i = wstate["n"] % 2
            wstate["n"] += 1
            v = src2d.rearrange("(k p) c -> p k c", p=128)
            for hf in range(2):
                P.dma("pool", WBk[i][:, hf * 8:(hf + 1) * 8, :], v[:, hf * 8:(hf + 1) * 8, :], writes=[("wb", i, hf)])
            return i

        def wload_fc(src2d):
            i = wstate["n"] % 2
            wstate["n"] += 1
            v = src2d.rearrange("(f p) c -> p f c", p=128)
            for hf in range(2):
                P.dma("pool", WBf[i][:, hf * 2:(hf + 1) * 2, :], v[:, hf * 2:(hf + 1) * 2, :], writes=[("wb", i, hf)])
            return i

        wkeys = lambda i: [("wb", i, 0), ("wb", i, 1)]

        def adaln_wload(vec_idx, cb):
            c0 = vec_idx * D + cb * 512
            return wload_kc(w_ada[:, c0:c0 + 512])

        def adaln_block(vec_idx, cb, i):
            c0 = vec_idx * D + cb * 512

            def mm(e, i=i):
                return [e.matmul(pbs[0:1, :], cact[:, k:k + 1], WBk[i][:, k, :], start=(k == 0), stop=(k == 15)) for k in range(16)]
            P.op("pe", mm, reads=["cact"] + wkeys(i), writes=["pbs"], banks=[PBS])
            P.dma("sp", brow[0:1, :], b_ada[0:1, c0:c0 + 512], writes=["brow"])
            P.op("dve", lambda e, cb=cb: e.tensor_tensor(gtbc[0:1, cb * 512:(cb + 1) * 512], pbs[0:1, :], brow[0:1, :], ALU.add),
                 reads=["pbs", "brow"], writes=["gtbc"], banks=[PBS])

        def adaln_row(vec_idx):
            for cb in range(4):
                adaln_block(vec_idx, cb, adaln_wload(vec_idx, cb))

        def adaln_cols(dst_col, gvec=None):
            def mm(e):
                return [e.matmul(pbs[:, k:k + 1], gtbc[0:1, k * 128:(k + 1) * 128], onef[0:1, 0:1], start=True, stop=True) for k in range(16)]
            P.op("pe", mm, reads=["gtbc", "onef"], writes=["pbs"], banks=[PBS])
            if gvec is None:
                P.op("dve", lambda e: e.tensor_copy(mods[:, dst_col:dst_col + 16], pbs[:, 0:16]), reads=["pbs"], writes=[("mods", dst_col)], banks=[PBS])
            else:
                P.op("dve", lambda e: e.scalar_tensor_tensor(mods[:, dst_col:dst_col + 16], pbs[:, 0:16], 1.0, V(gvec, 16), ALU.add, ALU.mult),
                     reads=["pbs", "vecs"], writes=[("mods", dst_col)], banks=[PBS])

        def adaln_bcast():
            for cb in range(4):
                P.op("pe", lambda e, cb=cb: e.matmul(pbx[:, :], onef[0:1, 0:128], gtbc[0:1, cb * 512:(cb + 1) * 512], start=True, stop=True),
                     reads=["gtbc", "onef"], writes=["pbx"], banks=[PBX])
                P.op("dve", lambda e, cb=cb: e.tensor_copy(gtbc[:, cb * 512:(cb + 1) * 512], pbx[:, :]), reads=["pbx"], writes=["gtbc"], banks=[PBX])

        c16 = sb("c16", [128, 256], FP16)
        P.op("dve", lambda e: e.tensor_copy(c16[:], cbf[:, 384:640]), reads=["cbf"], writes=["c16"])
        ntri16 = c16[:, 0:128]
        nones16 = c16[:, 128:256]
        identf = sb("identf", [128, 128])
        dg = sb("dg", [128, 256])
        P.op("dve", lambda e: e.tensor_copy(identf[:], ident), reads=["cbf"], writes=["identf"])

        def bcast_from_cols(col0):
            for kq in range(4):
                for kk in range(4):
                    k = kq * 4 + kk
                    ds = k % 2
                    dgv = dg[:, ds * 128:(ds + 1) * 128]
                    P.op("dve", lambda e, dgv=dgv, k=k: e.tensor_scalar(dgv, identf[:], mods[:, col0 + k:col0 + k + 1], None, ALU.mult),
                         reads=["identf", ("mods", col0)], writes=[("dg", ds)])
                    P.op("pe", lambda e, dgv=dgv, kk=kk: e.matmul(pbx[:, kk * 128:(kk + 1) * 128], onef[:, 0:128], dgv, start=True, stop=True),
                         reads=[("dg", ds), "onef"], writes=["pbx"], banks=[PBX])
                P.op("dve", lambda e, kq=kq: e.tensor_copy(gtbc[:, kq * 512:(kq + 1) * 512], pbx[:, :]), reads=["pbx"], writes=["gtbc"], banks=[PBX])

        adaln_row(1)
        adaln_cols(0, GMIX)
        adaln_row(0)
        adaln_cols(16)
        if stop == "setup":
            P.dma("sp", dbg["mods"], mods[:], reads=[("mods", 0), ("mods", 16)])
            P.emit_all(final_waits=list(P.pending_dma))
            return nc

        def norm_tile(src_ap, slot, from_dram, src_keys):
            if from_dram:
                P.dma("sp", xst[slot], src_ap, writes=[("xst", slot)])
                xin = xst[slot]
                rk = [("xst", slot)]
            else:
                xin = src_ap
                rk = list(src_keys)
            sq = ssq[:, slot:slot + 1]
            P.op("dve", lambda e: e.memset(sq, 0.0), writes=[("ssq", slot)])
            P.op("act", lambda e: e.activation(xn[slot], xin, AF.Square, accum_out=sq), reads=rk + [("ssq", slot)], writes=[("xn", slot), ("ssq", slot)])
            P.op("dve", lambda e: e.tensor_scalar(sq, sq, 1.0 / D, EPS, ALU.mult, ALU.add), reads=[("ssq", slot)], writes=[("ssq", slot)])
            rsqrt_inplace(sq, ("ssq", slot))
            P.op("dve", lambda e: e.tensor_scalar(xn[slot], xin, sq, None, ALU.mult), reads=rk + [("ssq", slot)], writes=[("xn", slot)])

        tcount = {"n": 0}

        def transpose_tile(slot, dst_fn, mcol, dst_keys):
            for half in range(2):
                bank = 4 + half
                def tr(e, half=half):
                    return [e.transpose(pstv[half][:, j * 128:(j + 1) * 128], xn[slot][:, (half * 8 + j) * 128:(half * 8 + j + 1) * 128], ident) for j in range(8)]
                P.op("pe", tr, reads=[("xn", slot), "cbf"], writes=[("pst", half)], banks=[bank])
                for j in range(8):
                    k = half * 8 + j
                    pv = pstv[half][:, j * 128:(j + 1) * 128]
                    sc = mods[:, mcol + k:mcol + k + 1]
                    bi = mods[:, mcol + 16 + k:mcol + 16 + k + 1]
                    if half == 0:
                        P.op("act", lambda e, k=k, pv=pv, sc=sc, bi=bi: e.activation(dst_fn(k), pv, AF.Identity, bias=bi, scale=sc),
                             reads=[("pst", half), ("mods", mcol), ("mods", mcol + 16)], writes=dst_keys(k), banks=[bank])
                    else:
                        P.op("dve", lambda e, k=k, pv=pv, sc=sc, bi=bi: e.tensor_scalar(dst_fn(k), pv, sc, bi, ALU.mult, ALU.add),
                             reads=[("pst", half), ("mods", mcol), ("mods", mcol + 16)], writes=dst_keys(k), banks=[bank])

        def make_hT_group(src_dram, g):
            for i in range(4):
                slot = i % 2
                norm_tile(src_dram[(g * 4 + i) * 128:(g * 4 + i + 1) * 128, :], slot, True, None)
                transpose_tile(slot, lambda k, i=i: hTg[:, k, i * 128:(i + 1) * 128], 0, lambda k, i=i: [("hTg", k, i)])

        hTg_keys = lambda k: [("hTg", k, i) for i in range(4)]
        acc_rr = {"n": 0}

        acc_n = {"n": 2}

        def next_acc():
            j = acc_rr["n"] % acc_n["n"]
            acc_rr["n"] += 1
            return big[j // 2][:, (j % 2) * 512:(j % 2 + 1) * 512], ("acc", j), j

        def proj_fm(wi, cl_, n_tok=512):
            acc, key, bank = next_acc()

            def mm(e):
                return [e.matmul(acc, WBk[wi][:, k, cl_ * 128:(cl_ + 1) * 128], hTg[:, k, :], start=(k == 0), stop=(k == 15)) for k in range(16)]
            P.op("pe", mm, reads=wkeys(wi) + [kk for k in range(16) for kk in hTg_keys(k)], writes=[key], banks=[bank])
            return acc, key, bank

        XR0 = 3072
        SETW = 2308

        def lru_head(g, h, wi, hl, st_):
            o0 = L0 + st_ * SETW
            xr = r3(o0, 515)
            xc = r3(o0 + 516, 512)
            rr_ = r3(o0 + 1028, 512)
            ii_ = r3(o0 + 1540, 512)
            xcb = r3(o0 + 2052, 256, BF16)
            tt_ = xr[:, 0:512]
            if st_ == 0:
                accx, bx = big[0][:, 0:512], 0
                gr, bgr = big[1][:, 0:512], 2
                gi, bgi = big[1][:, 512:1024], 3
            else:
                accx, bx = big[0][:, 512:1024], 1
                gr, bgr = pbs[:, :], PBS
                gi, bgi = pbx[:, :], PBX
            K = lambda n: (n, st_)

            def mm(e):
                return [e.matmul(accx, WBk[wi][:, k, hl * 128:(hl + 1) * 128], hTg[:, k, :], start=(k == 0), stop=(k == 15)) for k in range(16)]
            P.op("pe", mm, reads=wkeys(wi) + [kk for k in range(16) for kk in hTg_keys(k)], writes=[K("accx")], banks=[bx])
            P.op("dve", lambda e: e.tensor_copy(xr[:, 0:3], halo[:, h, :]), reads=["halo"], writes=[K("xrbuf")])
            P.op("act", lambda e: e.copy(xr[:, 3:515], accx), reads=[K("accx")], writes=[K("xrbuf")], banks=[bx])
            yield
            P.op("dve", lambda e: e.tensor_copy(halo[:, h, :], xr[:, 512:515]), reads=[K("xrbuf")], writes=["halo"])
            wcv = lambda j: V(WC + h * 4 + j)
            P.op("dve", lambda e: e.tensor_scalar(xc, xr[:, 0:512], wcv(0), V(BCONV + h), ALU.mult, ALU.add),
                 reads=[K("xrbuf"), "vecs"], writes=[K("xc")])
            for j in range(1, 4):
                P.op("dve", lambda e, j=j: e.scalar_tensor_tensor(xc, xr[:, j:j + 512], wcv(j), xc, ALU.mult, ALU.add),
                     reads=[K("xrbuf"), K("xc"), "vecs"], writes=[K("xc")])
            yield
            P.op("act", lambda e: e.copy(xcb, xc), reads=[K("xc")], writes=[K("xcb")])
            P.op("pe", lambda e: e.matmul(gr, wrg[:, h * 128:(h + 1) * 128], xcb, start=True, stop=True),
                 reads=["wrga", K("xcb")], writes=[K("gr")], banks=[bgr])
            P.op("pe", lambda e: e.matmul(gi, wrg[:, 1024 + h * 128:1024 + (h + 1) * 128], xcb, start=True, stop=True),
                 reads=["wrgx", K("xcb")], writes=[K("gi")], banks=[bgi])
            yield
            P.op("act", lambda e: e.activation(rr_, gr, AF.Sigmoid, bias=V(BRGA + h)), reads=[K("gr"), "vecs"], writes=[K("rr")], banks=[bgr])
            P.op("act", lambda e: e.activation(ii_, gi, AF.Sigmoid, bias=V(BRGX + h)), reads=[K("gi"), "vecs"], writes=[K("ii")], banks=[bgi])
            yield
            P.op("act", lambda e: e.activation(rr_, rr_, AF.Exp, scale=cl[:, h:h + 1]), reads=[K("rr"), "cl"], writes=[K("rr")])
            P.op("dve", lambda e: e.tensor_tensor(tt_, rr_, rr_, ALU.mult), reads=[K("rr")], writes=[K("xrbuf")])
            P.op("dve", lambda e: e.tensor_scalar(tt_, tt_, -1.0, 1.0, ALU.mult, ALU.add), reads=[K("xrbuf")], writes=[K("xrbuf")])
            yield
            P.op("act", lambda e: e.activation(tt_, tt_, AF.Ln), reads=[K("xrbuf")], writes=[K("xrbuf")])
            P.op("act", lambda e: e.activation(tt_, tt_, AF.Exp, scale=0.5), reads=[K("xrbuf")], writes=[K("xrbuf")])
            P.op("dve", lambda e: e.tensor_tensor(ii_, ii_, xc, ALU.mult), reads=[K("ii"), K("xc")], writes=[K("ii")])
            yield
            P.op("dve", lambda e: e.tensor_tensor(ii_, ii_, tt_, ALU.mult), reads=[K("ii"), K("xrbuf")], writes=[K("ii")])
            P.op("dve", lambda e: e.tensor_tensor_scan(xc, rr_, ii_, carry[:, h:h + 1], ALU.mult, ALU.add),
                 reads=[K("rr"), K("ii"), "carry"], writes=[K("xc")])
            P.op("dve", lambda e: e.tensor_copy(carry[:, h:h + 1], xc[:, 511:512]), reads=[K("xc")], writes=["carry"])
            yield
            hv = xc.rearrange("p (j q c) -> p j q c", j=2, q=2)
            ho = hown[:, h, g * 256:(g + 1) * 256].rearrange("p (j c) -> p j c", j=2)
            P.op("dve", lambda e: e.tensor_scalar(ho, hv[:, :, 0, :], V(SEL), None, ALU.mult), reads=[K("xc"), "vecs"], writes=[("hown", h, g)])
            P.op("dve", lambda e: e.scalar_tensor_tensor(ho, hv[:, :, 1, :], V(SEL + 1), ho, ALU.mult, ALU.add),
                 reads=[K("xc"), "vecs", ("hown", h, g)], writes=[("hown", h, g)])
            yield

        def run_pair(ga, gb_):
            gens = [ga, gb_]
            while gens:
                for gobj in list(gens):
                    try:
                        next(gobj)
                    except StopIteration:
                        gens.remove(gobj)

        for g in range(4):
            make_hT_group(x_all, g)
            for hb in range(2):
                wi = wload_kc(w_in[:, XR0 + hb * 512:XR0 + (hb + 1) * 512])
                for hp in range(2):
                    run_pair(lru_head(g, hb * 4 + hp * 2, wi, hp * 2, 0), lru_head(g, hb * 4 + hp * 2 + 1, wi, hp * 2 + 1, 1))
        if debug:
            P.dma("sp", dbg["mods"], mods[:], reads=[("mods", 0), ("mods", 16)])
            P.dma("sp", dbg["hown"], R1[:, 0:8192], reads=[("hown", h, g) for h in range(8) for g in range(4)])

        if stop == "L1":
            P.emit_all(final_waits=list(P.pending_dma))
            return nc
        P.barrier()
        gel = r3(10240, 512)
        osq = gtbc[:, 0:2048].bitcast(BF16).rearrange("p (h t) -> p h t", h=NH)
        hT2o = R3[:, 6144:10240].bitcast(BF16).rearrange("p (k t) -> p k t", k=16)
        obuf = [hTg, hT2o]

        def obuild(g, bi):
            for i in range(4):
                slot = i % 2
                norm_tile(x_own[(g * 4 + i) * 128:(g * 4 + i + 1) * 128, :], slot, True, None)
                yield
                transpose_tile(slot, lambda k, i=i: obuf[bi][:, k, i * 128:(i + 1) * 128], 0, lambda k, i=i: [("hTo", bi, k, i)])
                yield

        def oproj(g, bi):
            hk = [("hTo", bi, k, i) for k in range(16) for i in range(4)]

            def pf(wi, hl):
                acc, key, bank = next_acc()

                def mm(e):
                    return [e.matmul(acc, WBk[wi][:, k, hl * 128:(hl + 1) * 128], obuf[bi][:, k, :], start=(k == 0), stop=(k == 15)) for k in range(16)]
                P.op("pe", mm, reads=wkeys(wi) + hk, writes=[key], banks=[bank])
                return acc, key, bank

            for qb in range(2):
                wi = wload_kc(w_in[:, qb * 512:(qb + 1) * 512])
                for hl in range(4):
                    h = qb * 4 + hl
                    acc, akey, abank = pf(wi, hl)
                    if hl % 2 == 0:
                        P.op("act", lambda e, h=h, acc=acc: e.mul(qT[:, h, g * 512:(g + 1) * 512], acc, SCALE), reads=[akey], writes=[("qT", h, g)], banks=[abank])
                    else:
                        P.op("dve", lambda e, h=h, acc=acc: e.tensor_scalar(qT[:, h, g * 512:(g + 1) * 512], acc, SCALE, None, ALU.mult), reads=[akey], writes=[("qT", h, g)], banks=[abank])
                    yield
            for gb in range(2):
                wi = wload_kc(w_in[:, 4096 + gb * 512:4096 + (gb + 1) * 512])
                for hl in range(4):
                    h = gb * 4 + hl
                    acc, akey, abank = pf(wi, hl)
                    P.op("act", lambda e, acc=acc: e.activation(gel, acc, AF.Gelu_apprx_tanh), reads=[akey], writes=["gel"], banks=[abank])
                    ho = hown[:, h, g * 512:(g + 1) * 512]
                    P.op("dve", lambda e, ho=ho: e.tensor_tensor(ho, ho, gel, ALU.mult), reads=["gel", ("hown", h, 2 * g), ("hown", h, 2 * g + 1)],
                         writes=[("olru", h, g)])
                    P.op("act", lambda e, h=h, ho=ho: e.activation(mixL[:, h, g * 512:(g + 1) * 512], ho, AF.Identity, scale=V(GLRU + h)),
                         reads=[("olru", h, g), "vecs"], writes=[("mixL", h, g)])
                    P.op("act", lambda e, h=h, ho=ho: e.activation(osq[:, h, :], ho, AF.Square), reads=[("olru", h, g)], writes=[("osq", h)])
                    yield
            for i in range(4):
                tt = g * 4 + i

                def mm(e, i=i, tt=tt):
                    return [e.matmul(pbs[:, tt:tt + 1], osq[:, h, i * 128:(i + 1) * 128], ones_bf[:, 0:1], start=(h == 0), stop=(h == 7)) for h in range(8)]
                P.op("pe", mm, reads=[("osq", h) for h in range(8)] + ["cbf"], writes=[("pbs_s", tt)], banks=[PBS])
            yield

        for _ in obuild(0, 0):
            pass
        run_pair(oproj(0, 0), obuild(1, 1))
        for _ in oproj(1, 1):
            pass
        P.op("dve", lambda e: e.tensor_scalar(rstd_l, pbs[:, 0:8], 1.0 / 1024, EPS, ALU.mult, ALU.add), reads=[("pbs_s", t) for t in range(8)], writes=["rstd_l"], banks=[PBS])
        rsqrt_inplace(rstd_l, "rstd_l")
        if stop == "L2":
            P.dma("sp", dbg["r2"], R2[:, :], reads=[("mixL", h, g) for h in range(8) for g in range(2)] + [("qT", h, g) for h in range(8) for g in range(2)])
            P.dma("sp", dbg["st"][:, 0:16], small[:, 56:72], reads=["rstd_l"])
            P.emit_all(final_waits=list(P.pending_dma))
            return nc
        P.barrier()

        acc_n["n"] = 4
        hT2 = R3[:, 6144:10240].bitcast(BF16).rearrange("p (k t) -> p k t", k=16)
        hbuf = [hTg, hT2]

        def build_gen(g, bi):
            for i in range(4):
                slot = i % 2
                norm_tile(x_all[(g * 4 + i) * 128:(g * 4 + i + 1) * 128, :], slot, True, None)
                yield
                transpose_tile(slot, lambda k, i=i: hbuf[bi][:, k, i * 128:(i + 1) * 128], 0, lambda k, i=i: [("hTb", bi, k, i)])
                yield

        def kproj_gen(g, bi, part, wis):
            hk_all = [("hTb", bi, k, i) for k in range(16) for i in range(4)]
            if part == 0:
                for kb in range(2):
                    wi = wis[kb]
                    for hl in range(4):
                        h = kb * 4 + hl
                        acc, akey, abank = next_acc()

                        def mm(e, wi=wi, hl=hl, acc=acc):
                            return [e.matmul(acc, WBk[wi][:, k, hl * 128:(hl + 1) * 128], hbuf[bi][:, k, :], start=(k == 0), stop=(k == 15)) for k in range(16)]
                        P.op("pe", mm, reads=wkeys(wi) + hk_all, writes=[akey], banks=[abank])
                        if hl % 2 == 0:
                            P.op("act", lambda e, h=h, acc=acc: e.copy(kT[:, h, g * 512:(g + 1) * 512], acc), reads=[akey], writes=[("kT", h, g)], banks=[abank])
                        else:
                            P.op("dve", lambda e, h=h, acc=acc: e.tensor_copy(kT[:, h, g * 512:(g + 1) * 512], acc), reads=[akey], writes=[("kT", h, g)], banks=[abank])
                        yield
            else:
                for vb in range(2):
                    wi = wis[vb]
                    for i in range(4):
                        acc, akey, abank = next_acc()

                        def mm(e, i=i, acc=acc, wi=wi):
                            return [e.matmul(acc, hbuf[bi][:, k, i * 128:(i + 1) * 128], WBk[wi][:, k, :], start=(k == 0), stop=(k == 15)) for k in range(16)]
                        P.op("pe", mm, reads=wkeys(wi) + [("hTb", bi, k, i) for k in range(16)], writes=[akey], banks=[abank])
                        dst = Vv[:, g * 4 + i, vb * 512:(vb + 1) * 512]
                        if i % 2 == 0:
                            P.op("act", lambda e, dst=dst, acc=acc: e.copy(dst, acc), reads=[akey], writes=[("V", g * 4 + i, vb)], banks=[abank])
                        else:
                            P.op("dve", lambda e, dst=dst, acc=acc: e.tensor_copy(dst, acc), reads=[akey], writes=[("V", g * 4 + i, vb)], banks=[abank])
                        yield

        nb = 0
        for _ in build_gen(0, 0):
            pass
        for part in range(2):
            c0w = 1024 if part == 0 else 2048
            wis = [wload_kc(w_in[:, c0w + j * 512:c0w + (j + 1) * 512]) for j in range(2)]
            for g in range(4):
                last = (part == 1 and g == 3)
                nxt_g = (g + 1) % 4
                run_pair(kproj_gen(g, nb % 2, part, wis), build_gen(nxt_g, (nb + 1) % 2) if not last else iter(()))
                nb += 1
        if debug:
            P.dma("sp", dbg["kv"], R1[:, :], reads=[("kT", h, g) for h in range(8) for g in range(4)] + [("V", a, vb) for a in range(16) for vb in range(2)])
        if stop == "K":
            P.emit_all(final_waits=list(P.pending_dma))
            return nc
        P.barrier()

        P.op("dve", lambda e: e.memset(stat_a, 0.0), writes=["stat_a"])
        SCR = 2048

        def attn_stream(h, half, sl):
            o0 = sl * SCR
            e_ = r3(o0, 512)
            acc_ = r3(o0 + 512, 512)
            hi_ = r3(o0 + 1024, 256).bitcast(FP16)
            ahi_ = r3(o0 + 1280, 256).bitcast(FP16)
            w_ = r3(o0 + 1536, 256, BF16)
            sqa = r3(o0 + 1792, 256, BF16)
            Z = big[sl][:, 0:512]
            O = big[sl][:, 512:1024]
            bz, bo = 2 * sl, 2 * sl + 1
            K = lambda n: (n, sl)
            cbase = half * 512
            P.op("dve", lambda e: e.memset(acc_, 0.0), writes=[K("acc")])
            P.op("dve", lambda e: e.memset(ahi_, 0.0), writes=[K("ahi")])
            P.op("dve", lambda e: e.memset(O, 0.0), writes=[K("O")], banks=[bo])
            yield
            glist = range(7, -1, -1) if half == 0 else range(15, -1, -1)
            for g in glist:
                j0 = g // 2
                c0 = max(j0 * 128, cbase)
                diag = (j0 * 128 >= cbase)
                mcol = 0 if g % 2 == 0 else 128
                r = slice(c0 - cbase, 512)
                d = slice(c0 - cbase, c0 - cbase + 128)
                P.op("pe", lambda e, g=g, c0=c0, r=r: e.matmul(Z[:, r], kT[:, h, g * 128:(g + 1) * 128], qT[:, h, c0:cbase + 512], start=True, stop=True),
                     reads=[], writes=[K("Z")], banks=[bz])
                P.op("act", lambda e, r=r: e.activation(e_[:, r], Z[:, r], AF.Exp), reads=[K("Z")], writes=[K("e")], banks=[bz])
                yield
                P.op("act", lambda e, r=r: e.activation(hi_[:, r], e_[:, r], AF.Ln, bias=1.0), reads=[K("e")], writes=[K("hi")])
                yield
                if diag:
                    P.op("dve", lambda e, d=d, mcol=mcol: e.tensor_tensor(hi_[:, d], hi_[:, d], masks[:, mcol:mcol + 128], ALU.mult),
                         reads=[K("hi"), "masks"], writes=[K("hi")])
                    yield

                def mm(e, r=r):
                    return [e.matmul(Z[:, r], ntri16, hi_[:, r], start=False, stop=False, skip_group_check=True),
                            e.matmul(Z[:, r], nones16, ahi_[:, r], start=False, stop=False, skip_group_check=True)]
                P.op("pe", mm, reads=[K("hi"), K("ahi"), K("Z"), K("e"), "c16"], writes=[K("Z")], banks=[bz])
                yield
                P.op("act", lambda e, r=r: e.activation(w_[:, r], Z[:, r], AF.Exp), reads=[K("Z")], writes=[K("w")], banks=[bz])
                yield
                if diag:
                    P.op("dve", lambda e, d=d, mcol=mcol: e.tensor_tensor(w_[:, d], w_[:, d], masks_bf[:, mcol:mcol + 128], ALU.mult),
                         reads=[K("w"), "masks_bf"], writes=[K("w")])
                    yield
                P.op("pe", lambda e, g=g, r=r: e.matmul(O[:, r], Vv[:, g, h * 128:(h + 1) * 128], w_[:, r], start=False, stop=False, skip_group_check=True),
                     reads=[K("w"), K("O")], writes=[K("O")], banks=[bo])
                if g > 0:
                    P.op("dve", lambda e, r=r: e.tensor_tensor(acc_[:, r], acc_[:, r], hi_[:, r], ALU.add), reads=[K("acc"), K("hi")], writes=[K("acc")])
                    yield
                    P.op("dve", lambda e, r=r: e.tensor_copy(ahi_[:, r], acc_[:, r]), reads=[K("acc")], writes=[K("ahi")])
                yield
            P.op("act", lambda e: e.activation(mixA[:, h, cbase:cbase + 512], O, AF.Identity, scale=V(GATT + h)), reads=[K("O"), "vecs"], writes=[("mixA", h, half)], banks=[bo])
            P.op("act", lambda e: e.activation(sqa, O, AF.Square), reads=[K("O")], writes=[K("sqa")], banks=[bo])
            yield

            def mm2(e):
                return [e.matmul(pbs[:, 16 + half * 4 + i:16 + half * 4 + i + 1], sqa[:, i * 128:(i + 1) * 128], ones_bf[:, 0:1], start=True, stop=True) for i in range(4)]
            P.op("pe", mm2, reads=[K("sqa"), "cbf"], writes=["pbs_stat"], banks=[PBS])
            P.op("dve", lambda e: e.tensor_tensor(stat_a[:, half * 4:half * 4 + 4], stat_a[:, half * 4:half * 4 + 4], pbs[:, 16 + half * 4:16 + half * 4 + 4], ALU.add),
                 reads=["pbs_stat", "stat_a"], writes=["stat_a"], banks=[PBS])
            yield

        def adaln_stream():
            vecs_todo = ((2, 64, None), (4, 32, GMLP), (3, 48, None), (5, 80, None))
            blocks = [(vec, cb, col, gv) for (vec, col, gv) in vecs_todo for cb in range(4)]
            nxt = adaln_wload(blocks[0][0], blocks[0][1])
            yield
            for n, (vec, cb, col, gv) in enumerate(blocks):
                wi = nxt
                if n + 1 < len(blocks):
                    nxt = adaln_wload(blocks[n + 1][0], blocks[n + 1][1])
                yield
                adaln_block(vec, cb, wi)
                if cb == 3:
                    adaln_cols(col, gv)
                yield

        def run_streams(makers, n_slots, extra=(), extra_every=10):
            pending = list(makers)
            free = list(range(n_slots))
            active = []
            extra = list(extra)
            rnd = 0
            while pending or active or extra:
                rnd += 1
                while pending and free:
                    sl = free.pop(0)
                    active.append((pending.pop(0)(sl), sl))
                for item in list(active):
                    gobj, sl = item
                    try:
                        next(gobj)
                    except StopIteration:
                        active.remove(item)
                        free.append(sl)
                if rnd % extra_every == 0 or not (pending or active):
                    for gobj in list(extra):
                        try:
                            next(gobj)
                        except StopIteration:
                            extra.remove(gobj)

        order = [(h, 1) for h in range(NH)] + [(h, 0) for h in range(NH)]
        makers = [(lambda sl, h=h, half=half: attn_stream(h, half, sl)) for (h, half) in order]
        run_streams(makers, 3, extra=[adaln_stream()])
        P.op("dve", lambda e: e.tensor_scalar(rstd_a, stat_a, 1.0 / 1024, EPS, ALU.mult, ALU.add), reads=["stat_a"], writes=["rstd_a"])
        rsqrt_inplace(rstd_a, "rstd_a")
        if debug:
            P.dma("sp", dbg["r2"], R2[:, :], reads=[("mixA", h, half) for h in range(8) for half in range(2)])
            P.dma("sp", dbg["st"][:, 0:16], small[:, 56:72], reads=["rstd_a"])
        if stop == "T":
            P.emit_all(final_waits=list(P.pending_dma))
            return nc
        bcast_from_cols(64)

        def wout_block(cb):
            wi = wload_kc(w_out[:, cb * 512:(cb + 1) * 512])
            gb = gtbc[:, cb * 512:(cb + 1) * 512].unsqueeze(1).to_broadcast([128, 16, 512])
            P.op("dve", lambda e, wi=wi, gb=gb: e.tensor_tensor(WBk[wi][:, :, :], WBk[wi][:, :, :], gb, ALU.mult), reads=wkeys(wi) + ["gtbc"], writes=wkeys(wi))
            return wi
        pre_wi = [wout_block(0), wout_block(1)]
        P.barrier()

        for tt in range(8):
            P.dma("sp", x1[:, tt, :], x_own[tt * 128:(tt + 1) * 128, :], writes=[("x1", tt, cb) for cb in range(4)])
        for cb in range(4):
            wi = pre_wi[cb] if cb < 2 else wout_block(cb)
            for tt in range(8):
                accA, kA, bA = next_acc()
                accL, kL, bL = next_acc()

                def mm(e, tt=tt, wi=wi, accA=accA, accL=accL):
                    r = [e.matmul(accA, mixA[:, h, tt * 128:(tt + 1) * 128], WBk[wi][:, h, :], start=(h == 0), stop=(h == 7)) for h in range(8)]
                    r += [e.matmul(accL, mixL[:, h, tt * 128:(tt + 1) * 128], WBk[wi][:, 8 + h, :], start=(h == 0), stop=(h == 7)) for h in range(8)]
                    return r
                P.op("pe", mm, reads=wkeys(wi), writes=[kA, kL], banks=[bA, bL])
                xs = x1[:, tt, cb * 512:(cb + 1) * 512]
                P.op("dve", lambda e, xs=xs, accA=accA, tt=tt: e.scalar_tensor_tensor(xs, accA, rstd_a[:, tt:tt + 1], xs, ALU.mult, ALU.add),
                     reads=[kA, ("x1", tt, cb)], writes=[("x1", tt, cb)], banks=[bA])
                P.op("dve", lambda e, xs=xs, accL=accL, tt=tt: e.scalar_tensor_tensor(xs, accL, rstd_l[:, tt:tt + 1], xs, ALU.mult, ALU.add),
                     reads=[kL, ("x1", tt, cb)], writes=[("x1", tt, cb)], banks=[bL])
        if debug:
            P.dma("sp", dbg["x1"], R1[:, :], reads=[("x1", tt, cb) for tt in range(8) for cb in range(4)])
        if stop == "O":
            P.emit_all(final_waits=list(P.pending_dma))
            return nc
        P.barrier()
        for tt in range(8):
            slot = tt % 2
            norm_tile(x1[:, tt, :], slot, False, [("x1", tt, cb) for cb in range(4)])
            transpose_tile(slot, lambda k, tt=tt: h2T[:, k, tt * 128:(tt + 1) * 128], 32, lambda k, tt=tt: [("h2T", k, tt)])
        bcast_from_cols(80)

        rl = [r3(6144, 512), r3(6656, 512)]
        for fg in range(16):
            w1i = wload_kc(w1[:, fg * 512:(fg + 1) * 512])
            w2i = wload_fc(w2[fg * 512:(fg + 1) * 512, :])
            gb = gtbc[:, :].unsqueeze(1).to_broadcast([128, 4, 2048])
            P.op("dve", lambda e, w2i=w2i, gb=gb: e.tensor_tensor(WBf[w2i][:, :, :], WBf[w2i][:, :, :], gb, ALU.mult), reads=wkeys(w2i) + ["gtbc"], writes=wkeys(w2i))
            for fc in range(4):
                for tg in range(2):
                    acc, akey, abank = next_acc()

                    def mm(e, fc=fc, tg=tg, acc=acc, w1i=w1i):
                        return [e.matmul(acc, WBk[w1i][:, k, fc * 128:(fc + 1) * 128], h2T[:, k, tg * 512:(tg + 1) * 512], start=(k == 0), stop=(k == 15)) for k in range(16)]
                    P.op("pe", mm, reads=wkeys(w1i) + [("h2T", k, t) for k in range(16) for t in range(tg * 4, tg * 4 + 4)], writes=[akey], banks=[abank])
                    s = (fc * 2 + tg) % 2
                    P.op("act", lambda e, s=s, acc=acc: e.activation(rl[s], acc, AF.Relu), reads=[akey], writes=[("rl", s)], banks=[abank])
                    P.op("dve", lambda e, s=s, fc=fc, tg=tg: e.tensor_tensor(hid[:, fc, tg * 512:(tg + 1) * 512], rl[s], rl[s], ALU.mult),
                         reads=[("rl", s)], writes=[("hid", fc, tg)])
            for tt in range(8):
                for cb in range(4):
                    acc, akey, abank = next_acc()

                    def mm(e, tt=tt, cb=cb, acc=acc, w2i=w2i):
                        return [e.matmul(acc, hid[:, fc, tt * 128:(tt + 1) * 128], WBf[w2i][:, fc, cb * 512:(cb + 1) * 512], start=(fc == 0), stop=(fc == 3)) for fc in range(4)]
                    P.op("pe", mm, reads=wkeys(w2i) + [("hid", fc, tt // 4) for fc in range(4)], writes=[akey], banks=[abank])
                    xs = x1[:, tt, cb * 512:(cb + 1) * 512]
                    P.op("dve", lambda e, xs=xs, acc=acc: e.tensor_tensor(xs, xs, acc, ALU.add), reads=[akey, ("x1", tt, cb)], writes=[("x1", tt, cb)], banks=[abank])

        P.dma("sp", gtbc[:], gfin_d, writes=["gtbc"])
        ob = [r3(1024, 2048), r3(3072, 2048)]
        junk = r3(5120, 1024, BF16)
        outs = []
        for tt in range(8):
            s = tt % 2
            sq = ssq[:, 2 + s:3 + s]
            xk = [("x1", tt, cb) for cb in range(4)]
            P.op("dve", lambda e, sq=sq: e.memset(sq, 0.0), writes=[("fsq", s)])
            P.op("act", lambda e, sq=sq, tt=tt: e.activation(junk, x1[:, tt, :], AF.Square, accum_out=sq), reads=xk + [("fsq", s)], writes=["junk", ("fsq", s)])
            P.op("dve", lambda e, sq=sq: e.tensor_scalar(sq, sq, 1.0 / D, EPS, ALU.mult, ALU.add), reads=[("fsq", s)], writes=[("fsq", s)])
            rsqrt_inplace(sq, ("fsq", s))
            P.op("dve", lambda e, sq=sq, tt=tt, s=s: e.scalar_tensor_tensor(ob[s], x1[:, tt, :], sq, gtbc[:], ALU.mult, ALU.mult),
                 reads=xk + [("fsq", s), "gtbc"], writes=[("ob", s)])
            outs.append(P.dma("sp", out_d[tt * 128:(tt + 1) * 128, :], ob[s], reads=[("ob", s)]))
        fw = list(outs) + [o for o in P.pending_dma if o not in outs]
        P.emit_all(final_waits=fw)
    return nc


def _core_inputs(inp, b, p):
    f = np.float32
    x = np.ascontiguousarray(inp["x"][b], dtype=f)
    xb = x.reshape(16, 128, D)
    x_own = np.ascontiguousarray(xb[p::2].reshape(1024, D))
    col = lambda v, n: np.ascontiguousarray(np.asarray(v, dtype=f).reshape(n, 128).T)
    vecs = np.zeros((128, NV), dtype=f)
    vecs[:, GMIX:GMIX + 16] = col(inp["g_norm_mix"][0], 16)
    vecs[:, GMLP:GMLP + 16] = col(inp["g_norm_mlp"][0], 16)
    vecs[:, CT:CT + 16] = col(inp["c"][b], 16)
    wc = np.asarray(inp["w_conv"][0], dtype=f)
    vecs[:, WC:WC + 32] = wc.reshape(4, 8, 128).transpose(2, 1, 0).reshape(128, 32)
    vecs[:, BCONV:BCONV + 8] = col(inp["b_conv"][0], 8)
    vecs[:, BRGA:BRGA + 8] = col(inp["b_rg_a"][0], 8)
    vecs[:, BRGX:BRGX + 8] = col(inp["b_rg_x"][0], 8)
    vecs[:, LAM:LAM + 8] = col(inp["lru_lambda"][0], 8)
    vecs[:, GATT:GATT + 8] = col(inp["g_attn_out"][0], 8)
    vecs[:, GLRU:GLRU + 8] = col(inp["g_lru_out"][0], 8)
    vecs[:, SEL] = 1.0 if p == 0 else 0.0
    vecs[:, SEL + 1] = 0.0 if p == 0 else 1.0
    s_idx = np.arange(128)[:, None]
    t_idx = np.arange(128)[None, :]
    tri_mask = (s_idx < t_idx).astype(f)
    if p == 0:
        masks = np.concatenate([tri_mask, np.zeros((128, 128), f)], axis=1)
    else:
        masks = np.concatenate([np.ones((128, 128), f), tri_mask], axis=1)
    bf = ml_dtypes.bfloat16
    cbf = np.concatenate([np.eye(128, dtype=f), (s_idx > t_idx).astype(f), np.ones((128, 128), f),
                          -(s_idx >= t_idx).astype(f), -np.ones((128, 128), f)], axis=1).astype(bf)
    return {
        "x_all": x, "x_own": x_own, "vecs": vecs,
        "b_ada": np.ascontiguousarray(np.asarray(inp["b_ada"], dtype=f).reshape(1, 6 * D)),
        "masks": np.ascontiguousarray(masks),
        "gfin": np.ascontiguousarray(np.broadcast_to(np.asarray(inp["g_norm_final"], dtype=f)[None, :], (128, D))),
        "cbf": np.ascontiguousarray(cbf),
        "w_ada": np.ascontiguousarray(inp["w_ada"][0], dtype=f),
        "w_in": np.ascontiguousarray(inp["w_in"][0], dtype=f),
        "w_rga": np.ascontiguousarray(inp["w_rg_a"][0], dtype=f),
        "w_rgx": np.ascontiguousarray(inp["w_rg_x"][0], dtype=f),
        "w_out": np.ascontiguousarray(inp["w_out"][0], dtype=f),
        "w1": np.ascontiguousarray(inp["w_mlp_in"][0], dtype=f),
        "w2": np.ascontiguousarray(inp["w_mlp_out"][0], dtype=f),
    }


_NC_CACHE = {}


def kernel(**inputs):
    inp = {k: np.asarray(v) for k, v in inputs.items()}
    if "nc" not in _NC_CACHE:
        _NC_CACHE["nc"] = build(False)
    nc = _NC_CACHE["nc"]
    in_maps = [_core_inputs(inp, c // 2, c % 2) for c in range(8)]
    res = run_bass_kernel_spmd(nc, in_maps, core_ids=list(range(8)))
    out = np.zeros((4, 16, 128, D), dtype=np.float32)
    for c in range(8):
        b, p = c // 2, c % 2
        out[b, p::2] = np.asarray(res.results[c]["out"], dtype=np.float32).reshape(8, 128, D)
    return out.reshape(4, S, D)
```

```python
import contextlib
import numpy as np
import ml_dtypes
import concourse.bass as bass
import concourse.mybir as mybir
from concourse.bass_utils import run_bass_kernel_spmd

F32 = mybir.dt.float32
BF16 = mybir.dt.bfloat16
FP16 = mybir.dt.float16
AF = mybir.ActivationFunctionType
ALU = mybir.AluOpType

D = 2048
S = 2048
NH = 8
DFF = 8192
EPS = 1e-6
SCALE = 128.0 ** -0.5
LRU_C = 8.0
SEM_CAP = 30000
HILO = True

GMIX, GMLP, CT, WC, BCONV, BRGA, BRGX, LAM, GATT, GLRU, SEL, NV = 0, 16, 32, 48, 80, 88, 96, 104, 112, 120, 128, 130


class Op:
    __slots__ = ("eng", "emit", "deps", "signal", "sig_no", "idx", "dma_sem", "dma_val", "pre")

    def __init__(self, eng, emit):
        self.eng = eng
        self.emit = emit
        self.deps = []
        self.signal = False
        self.sig_no = None
        self.idx = None
        self.dma_sem = None
        self.dma_val = None
        self.pre = None


class Prog:
    ENGS = ("pe", "act", "dve", "pool", "sp")

    def __init__(self, nc, n_dma_sems=32):
        self.nc = nc
        self.ops = {e: [] for e in self.ENGS}
        self.last_writer = {}
        self.readers = {}
        self.n_dma_sems = n_dma_sems
        self.dma_rr = {"sp": 0, "pool": 0, "act": 0}
        self.dma_counts = [0] * n_dma_sems
        self.dma_last = [None] * n_dma_sems
        self.all_ops = []
        self.pending_dma = []
        self.last_op = {}
        self.bank_last = {}

    def _add_dep(self, sel, d):
        if d.dma_sem is not None:
            sel[("dma", id(d))] = d
        else:
            cur = sel.get(d.eng)
            if cur is None or cur.idx < d.idx:
                sel[d.eng] = d

    def _track(self, op, reads, writes):
        sel = {}
        for r in reads:
            w = self.last_writer.get(r)
            if w is not None:
                self._add_dep(sel, w)
        for k in writes:
            w = self.last_writer.get(k)
            if w is not None:
                self._add_dep(sel, w)
            for rd in self.readers.get(k, {}).values():
                self._add_dep(sel, rd)
        for r in reads:
            rr = self.readers.setdefault(r, {})
            key = op.eng if op.dma_sem is None else ("dma", id(op))
            rr[key] = op
        for k in writes:
            self.last_writer[k] = op
            self.readers[k] = {}
        op.deps = [d for d in sel.values() if d is not op]

    def _new(self, eng, emit):
        o = Op(eng, emit)
        o.idx = len(self.ops[eng])
        self.ops[eng].append(o)
        self.all_ops.append(o)
        self.last_op[eng] = o
        return o

    def op(self, eng, emit, reads=(), writes=(), banks=()):
        o = self._new(eng, emit)
        self._track(o, reads, writes)
        for b in banks:
            last = self.bank_last.get(b)
            if last is not None and last.eng != eng:
                cur = [d for d in o.deps if d.eng == last.eng and d.dma_sem is None]
                if not cur:
                    o.deps.append(last)
                elif cur[0].idx < last.idx:
                    o.deps.remove(cur[0])
                    o.deps.append(last)
            self.bank_last[b] = o
        return o

    def dma(self, queue, out_ap, in_ap, reads=(), writes=()):
        half = self.n_dma_sems // 2
        base = 0 if queue == "sp" else half
        s = base + self.dma_rr[queue]
        self.dma_rr[queue] = (self.dma_rr[queue] + 1) % half
        o = self._new(queue, lambda e, a=out_ap, b=in_ap: e.dma_start(out=a, in_=b))
        o.pre = self.dma_last[s]
        self.dma_counts[s] += 1
        o.dma_sem = s
        o.dma_val = 16 * self.dma_counts[s]
        self.dma_last[s] = o
        self.pending_dma.append(o)
        self._track(o, reads, writes)
        return o

    def barrier(self):
        deps = [o for e, o in self.last_op.items() if o.dma_sem is None and o.emit is not None]
        deps += self.pending_dma
        self.pending_dma = []
        for e in self.ENGS:
            o = self._new(e, None)
            o.deps = [d for d in deps]
        self.last_writer = {}
        self.readers = {}
        self.bank_last = {}

    def emit_all(self, final_waits=()):
        nc = self.nc
        for o in self.all_ops:
            for d in o.deps:
                if d.dma_sem is None:
                    d.signal = True
        n_sig = {}
        for e in self.ENGS:
            c = 0
            for o in self.ops[e]:
                if o.dma_sem is None and o.signal:
                    assert o.emit is not None
                    c += 1
                    o.sig_no = c
            n_sig[e] = c
        with contextlib.ExitStack() as st:
            eng_sems = {}
            for e in self.ENGS:
                k = max(1, (n_sig[e] + SEM_CAP - 1) // SEM_CAP)
                eng_sems[e] = [st.enter_context(nc.semaphore(f"s_{e}_{i}")) for i in range(k)]
            dma_sems = [st.enter_context(nc.semaphore(f"s_dma_{i}")) for i in range(self.n_dma_sems)]
            block = st.enter_context(nc.Block())

            def waits_for(deps, state):
                best = {}
                for d in deps:
                    if d.dma_sem is not None:
                        key = ("dma", d.dma_sem)
                        sem = dma_sems[d.dma_sem]
                        val = d.dma_val
                    else:
                        si = (d.sig_no - 1) // SEM_CAP
                        key = (d.eng, si)
                        sem = eng_sems[d.eng][si]
                        val = (d.sig_no - 1) % SEM_CAP + 1
                    if state.get(key, 0) >= val:
                        continue
                    state[key] = val
                    best[key] = (sem, val)
                return list(best.values())

            def run_engine(ename, handle):
                state = {}
                for o in self.ops[ename]:
                    deps = list(o.deps)
                    if o.pre is not None:
                        deps.append(o.pre)
                    for sem, val in waits_for(deps, state):
                        handle.wait_ge(sem, val)
                    if o.emit is None:
                        continue
                    r = o.emit(handle)
                    last = r[-1] if isinstance(r, (list, tuple)) else r
                    if o.dma_sem is not None:
                        last.then_inc(dma_sems[o.dma_sem], 16)
                    elif o.signal:
                        si = (o.sig_no - 1) // SEM_CAP
                        last.then_inc(eng_sems[ename][si], 1)
                if ename == "sp":
                    for sem, val in waits_for(list(final_waits), state):
                        handle.wait_ge(sem, val)

            block.tensor(lambda h: run_engine("pe", h))
            block.scalar(lambda h: run_engine("act", h))
            block.vector(lambda h: run_engine("dve", h))
            block.gpsimd(lambda h: run_engine("pool", h))
            block.sync(lambda h: run_engine("sp", h))


def build(debug=False, stop=None):
    nc = bass.Bass("TRN2", target_bir_lowering=False)
    dram = lambda n, s, dt=F32, kind="ExternalInput": nc.dram_tensor(n, s, dt, kind=kind).ap()
    x_all = dram("x_all", [S, D])
    x_own = dram("x_own", [S // 2, D])
    vecs_d = dram("vecs", [128, NV])
    b_ada = dram("b_ada", [1, 6 * D])
    masks_d = dram("masks", [128, 256])
    gfin_d = dram("gfin", [128, D])
    cbf_d = dram("cbf", [128, 640], BF16)
    w_ada = dram("w_ada", [D, 6 * D])
    w_in = dram("w_in", [D, 5120])
    w_rga = dram("w_rga", [NH, 128, 128])
    w_rgx = dram("w_rgx", [NH, 128, 128])
    w_out = dram("w_out", [D, D])
    w1 = dram("w1", [D, DFF])
    w2 = dram("w2", [DFF, D])
    out_d = dram("out", [S // 2, D], F32, "ExternalOutput")
    dbg = {}
    if debug:
        dbg["mods"] = dram("d_mods", [128, 64], F32, "ExternalOutput")
        dbg["hown"] = dram("d_hown", [128, 8192], F32, "ExternalOutput")
        dbg["kv"] = dram("d_kv", [128, 16384], F32, "ExternalOutput")
        dbg["r2"] = dram("d_r2", [128, 12288], F32, "ExternalOutput")
        dbg["st"] = dram("d_st", [128, 32], F32, "ExternalOutput")
        dbg["x1"] = dram("d_x1", [128, 16384], F32, "ExternalOutput")

    with contextlib.ExitStack() as st:
        sb = lambda n, s, dt=F32: st.enter_context(nc.sbuf_tensor("sb_" + n, s, dt))
        R1 = sb("R1", [128, 16384])
        R2 = sb("R2", [128, 12288])
        R3 = sb("R3", [128, 10880])
        WB = [sb(f"WB{i}", [128, 8192], BF16) for i in range(2)]
        vecs = sb("vecs", [128, NV])
        cbf = sb("cbf", [128, 640], BF16)
        masks = sb("masks", [128, 256])
        masks_bf = sb("masks_bf", [128, 256], BF16)
        gtbc = sb("gtbc", [128, D])
        mods = sb("mods", [128, 96])
        wrg = sb("wrg", [128, 2 * NH * 128], BF16)
        small = sb("small", [128, 96])
        cact = sb("cact", [128, 16], BF16)
        onef = sb("onef", [128, 128])
        brow = sb("brow", [1, 512])
        big = [st.enter_context(nc.psum_tensor(f"big{i}", [128, 1024], F32)) for i in range(3)]
        pbs = st.enter_context(nc.psum_tensor("pbs", [128, 512], F32))
        pbx = st.enter_context(nc.psum_tensor("pbx", [128, 512], F32))
        PBS, PBX = 6, 7
        pstv = [big[2][:, 0:512].bitcast(BF16), big[2][:, 512:1024].bitcast(BF16)]

        ident = cbf[:, 0:128]
        tri = cbf[:, 128:256]
        ones_bf = cbf[:, 256:384]
        ntri = cbf[:, 384:512]
        nones = cbf[:, 512:640]
        kT = R1[:, 0:8192].bitcast(BF16).rearrange("p (h t) -> p h t", h=NH)
        Vv = R1[:, 8192:16384].bitcast(BF16).rearrange("p (a c) -> p a c", a=16)
        hown = R1[:, 0:8192].rearrange("p (h t) -> p h t", h=NH)
        x1 = R1[:, :].rearrange("p (a d) -> p a d", a=8)
        qT = R2[:, 0:4096].bitcast(BF16).rearrange("p (h t) -> p h t", h=NH)
        mixA = R2[:, 4096:8192].bitcast(BF16).rearrange("p (h t) -> p h t", h=NH)
        mixL = R2[:, 8192:12288].bitcast(BF16).rearrange("p (h t) -> p h t", h=NH)
        hTg = R2[:, 4096:8192].bitcast(BF16).rearrange("p (k t) -> p k t", k=16)
        h2T = R2[:, 0:8192].bitcast(BF16).rearrange("p (k t) -> p k t", k=16)
        hid = R2[:, 8192:10240].bitcast(BF16).rearrange("p (f t) -> p f t", f=4)
        WBk = [w[:, :].rearrange("p (k c) -> p k c", k=16) for w in WB]
        WBf = [w[:, :].rearrange("p (f c) -> p f c", f=4) for w in WB]
        xst = [R3[:, 0:2048], R3[:, 2048:4096]]
        xn = [R3[:, 4096:5120].bitcast(BF16), R3[:, 5120:6144].bitcast(BF16)]
        L0 = 6144

        def r3(off, n, dt=F32):
            a = R3[:, off:off + n]
            return a if dt == F32 else a.bitcast(BF16)

        P = Prog(nc)

        def rsqrt_inplace(ap, key):
            P.op("act", lambda e: e.activation(ap, ap, AF.Ln), reads=[key], writes=[key])
            P.op("act", lambda e: e.activation(ap, ap, AF.Exp, scale=-0.5), reads=[key], writes=[key])
        V = lambda c0, n=1: vecs[:, c0:c0 + n]

        P.dma("sp", vecs[:], vecs_d, writes=["vecs"])
        P.dma("sp", cbf[:], cbf_d, writes=["cbf"])
        P.dma("sp", masks[:], masks_d, writes=["masks"])
        P.dma("pool", wrg[:, 0:1024].rearrange("p (h d) -> p h d", h=NH), w_rga.rearrange("h c d -> c h d"), writes=["wrga"])
        P.dma("pool", wrg[:, 1024:2048].rearrange("p (h d) -> p h d", h=NH), w_rgx.rearrange("h c d -> c h d"), writes=["wrgx"])
        P.op("dve", lambda e: e.memset(onef[:], 1.0), writes=["onef"])
        P.op("dve", lambda e: e.tensor_copy(masks_bf[:], masks[:]), reads=["masks"], writes=["masks_bf"])
        P.op("act", lambda e: e.activation(cact[:], V(CT, 16), AF.Silu), reads=["vecs"], writes=["cact"])
        cl = small[:, 0:8]
        P.op("act", lambda e: e.activation(small[:, 8:16], V(LAM, 8), AF.Exp, scale=-1.0), reads=["vecs"], writes=["cl_t"])
        P.op("act", lambda e: e.activation(small[:, 8:16], small[:, 8:16], AF.Ln, bias=1.0), reads=["cl_t"], writes=["cl_t"])
        P.op("dve", lambda e: e.tensor_scalar(cl, small[:, 8:16], -LRU_C, None, ALU.mult), reads=["cl_t"], writes=["cl"])
        carry = small[:, 16:24]
        P.op("dve", lambda e: e.memset(carry, 0.0), writes=["carry"])
        halo = small[:, 24:48].rearrange("p (h c) -> p h c", h=NH)
        P.op("dve", lambda e: e.memset(small[:, 24:48], 0.0), writes=["halo"])
        ssq = small[:, 48:56]
        rstd_a = small[:, 56:64]
        rstd_l = small[:, 64:72]
        stat_a = small[:, 72:80]
        rs_t = small[:, 80:88]

        wstate = {"n": 0}

        def wload_kc(src2d):
            i = wstate["n"] % 2
            wstate["n"] += 1
            v = src2d.rearrange("(k p) c -> p k c", p=128)
            for hf in range(2):
                P.dma("pool", WBk[i][:, hf * 8:(hf + 1) * 8, :], v[:, hf * 8:(hf + 1) * 8, :], writes=[("wb", i, hf)])
            return i

        def wload_fc(src2d):
            i = wstate["n"] % 2
            wstate["n"] += 1
            v = src2d.rearrange("(f p) c -> p f c", p=128)
            for hf in range(2):
                P.dma("pool", WBf[i][:, hf * 2:(hf + 1) * 2, :], v[:, hf * 2:(hf + 1) * 2, :], writes=[("wb", i, hf)])
            return i

        wkeys = lambda i: [("wb", i, 0), ("wb", i, 1)]

        def adaln_wload(vec_idx, cb):
            c0 = vec_idx * D + cb * 512
            return wload_kc(w_ada[:, c0:c0 + 512])

        def adaln_block(vec_idx, cb, i):
            c0 = vec_idx * D + cb * 512

            def mm(e, i=i):
                return [e.matmul(pbs[0:1, :], cact[:, k:k + 1], WBk[i][:, k, :], start=(k == 0), stop=(k == 15)) for k in range(16)]
            P.op("pe", mm, reads=["cact"] + wkeys(i), writes=["pbs"], banks=[PBS])
            P.dma("sp", brow[0:1, :], b_ada[0:1, c0:c0 + 512], writes=["brow"])
            P.op("dve", lambda e, cb=cb: e.tensor_tensor(gtbc[0:1, cb * 512:(cb + 1) * 512], pbs[0:1, :], brow[0:1, :], ALU.add),
                 reads=["pbs", "brow"], writes=["gtbc"], banks=[PBS])

        def adaln_row(vec_idx):
            for cb in range(4):
                adaln_block(vec_idx, cb, adaln_wload(vec_idx, cb))

        def adaln_cols(dst_col, gvec=None):
            def mm(e):
                return [e.matmul(pbs[:, k:k + 1], gtbc[0:1, k * 128:(k + 1) * 128], onef[0:1, 0:1], start=True, stop=True) for k in range(16)]
            P.op("pe", mm, reads=["gtbc", "onef"], writes=["pbs"], banks=[PBS])
            if gvec is None:
                P.op("dve", lambda e: e.tensor_copy(mods[:, dst_col:dst_col + 16], pbs[:, 0:16]), reads=["pbs"], writes=[("mods", dst_col)], banks=[PBS])
            else:
                P.op("dve", lambda e: e.scalar_tensor_tensor(mods[:, dst_col:dst_col + 16], pbs[:, 0:16], 1.0, V(gvec, 16), ALU.add, ALU.mult),
                     reads=["pbs", "vecs"], writes=[("mods", dst_col)], banks=[PBS])

        def adaln_bcast():
            for cb in range(4):
                P.op("pe", lambda e, cb=cb: e.matmul(pbx[:, :], onef[0:1, 0:128], gtbc[0:1, cb * 512:(cb + 1) * 512], start=True, stop=True),
                     reads=["gtbc", "onef"], writes=["pbx"], banks=[PBX])
                P.op("dve", lambda e, cb=cb: e.tensor_copy(gtbc[:, cb * 512:(cb + 1) * 512], pbx[:, :]), reads=["pbx"], writes=["gtbc"], banks=[PBX])

        c16 = sb("c16", [128, 256], FP16)
        P.op("dve", lambda e: e.tensor_copy(c16[:], cbf[:, 384:640]), reads=["cbf"], writes=["c16"])
        ntri16 = c16[:, 0:128]
        nones16 = c16[:, 128:256]
        identf = sb("identf", [128, 128])
        dg = sb("dg", [128, 256])
        P.op("dve", lambda e: e.tensor_copy(identf[:], ident), reads=["cbf"], writes=["identf"])

        def bcast_from_cols(col0):
            for kq in range(4):
                for kk in range(4):
                    k = kq * 4 + kk
                    ds = k % 2
                    dgv = dg[:, ds * 128:(ds + 1) * 128]
                    P.op("dve", lambda e, dgv=dgv, k=k: e.tensor_scalar(dgv, identf[:], mods[:, col0 + k:col0 + k + 1], None, ALU.mult),
                         reads=["identf", ("mods", col0)], writes=[("dg", ds)])
                    P.op("pe", lambda e, dgv=dgv, kk=kk: e.matmul(pbx[:, kk * 128:(kk + 1) * 128], onef[:, 0:128], dgv, start=True, stop=True),
                         reads=[("dg", ds), "onef"], writes=["pbx"], banks=[PBX])
                P.op("dve", lambda e, kq=kq: e.tensor_copy(gtbc[:, kq * 512:(kq + 1) * 512], pbx[:, :]), reads=["pbx"], writes=["gtbc"], banks=[PBX])

        adaln_row(1)
        adaln_cols(0, GMIX)
        adaln_row(0)
        adaln_cols(16)
        if stop == "setup":
            P.dma("sp", dbg["mods"], mods[:], reads=[("mods", 0), ("mods", 16)])
            P.emit_all(final_waits=list(P.pending_dma))
            return nc

        def norm_tile(src_ap, slot, from_dram, src_keys):
            if from_dram:
                P.dma("sp", xst[slot], src_ap, writes=[("xst", slot)])
                xin = xst[slot]
                rk = [("xst", slot)]
            else:
                xin = src_ap
                rk = list(src_keys)
            sq = ssq[:, slot:slot + 1]
            P.op("dve", lambda e: e.memset(sq, 0.0), writes=[("ssq", slot)])
            P.op("act", lambda e: e.activation(xn[slot], xin, AF.Square, accum_out=sq), reads=rk + [("ssq", slot)], writes=[("xn", slot), ("ssq", slot)])
            P.op("dve", lambda e: e.tensor_scalar(sq, sq, 1.0 / D, EPS, ALU.mult, ALU.add), reads=[("ssq", slot)], writes=[("ssq", slot)])
            rsqrt_inplace(sq, ("ssq", slot))
            P.op("dve", lambda e: e.tensor_scalar(xn[slot], xin, sq, None, ALU.mult), reads=rk + [("ssq", slot)], writes=[("xn", slot)])

        tcount = {"n": 0}

        def transpose_tile(slot, dst_fn, mcol, dst_keys):
            for half in range(2):
                bank = 4 + half
                def tr(e, half=half):
                    return [e.transpose(pstv[half][:, j * 128:(j + 1) * 128], xn[slot][:, (half * 8 + j) * 128:(half * 8 + j + 1) * 128], ident) for j in range(8)]
                P.op("pe", tr, reads=[("xn", slot), "cbf"], writes=[("pst", half)], banks=[bank])
                for j in range(8):
                    k = half * 8 + j
                    pv = pstv[half][:, j * 128:(j + 1) * 128]
                    sc = mods[:, mcol + k:mcol + k + 1]
                    bi = mods[:, mcol + 16 + k:mcol + 16 + k + 1]
                    if half == 0:
                        P.op("act", lambda e, k=k, pv=pv, sc=sc, bi=bi: e.activation(dst_fn(k), pv, AF.Identity, bias=bi, scale=sc),
                             reads=[("pst", half), ("mods", mcol), ("mods", mcol + 16)], writes=dst_keys(k), banks=[bank])
                    else:
                        P.op("dve", lambda e, k=k, pv=pv, sc=sc, bi=bi: e.tensor_scalar(dst_fn(k), pv, sc, bi, ALU.mult, ALU.add),
                             reads=[("pst", half), ("mods", mcol), ("mods", mcol + 16)], writes=dst_keys(k), banks=[bank])

        def make_hT_group(src_dram, g):
            for i in range(4):
                slot = i % 2
                norm_tile(src_dram[(g * 4 + i) * 128:(g * 4 + i + 1) * 128, :], slot, True, None)
                transpose_tile(slot, lambda k, i=i: hTg[:, k, i * 128:(i + 1) * 128], 0, lambda k, i=i: [("hTg", k, i)])

        hTg_keys = lambda k: [("hTg", k, i) for i in range(4)]
        acc_rr = {"n": 0}

        acc_n = {"n": 2}

        def next_acc():
            j = acc_rr["n"] % acc_n["n"]
            acc_rr["n"] += 1
            return big[j // 2][:, (j % 2) * 512:(j % 2 + 1) * 512], ("acc", j), j

        def proj_fm(wi, cl_, n_tok=512):
            acc, key, bank = next_acc()

            def mm(e):
                return [e.matmul(acc, WBk[wi][:, k, cl_ * 128:(cl_ + 1) * 128], hTg[:, k, :], start=(k == 0), stop=(k == 15)) for k in range(16)]
            P.op("pe", mm, reads=wkeys(wi) + [kk for k in range(16) for kk in hTg_keys(k)], writes=[key], banks=[bank])
            return acc, key, bank

        XR0 = 3072
        SETW = 2308

        def lru_head(g, h, wi, hl, st_):
            o0 = L0 + st_ * SETW
            xr = r3(o0, 515)
            xc = r3(o0 + 516, 512)
            rr_ = r3(o0 + 1028, 512)
            ii_ = r3(o0 + 1540, 512)
            xcb = r3(o0 + 2052, 256, BF16)
            tt_ = xr[:, 0:512]
            if st_ == 0:
                accx, bx = big[0][:, 0:512], 0
                gr, bgr = big[1][:, 0:512], 2
                gi, bgi = big[1][:, 512:1024], 3
            else:
                accx, bx = big[0][:, 512:1024], 1
                gr, bgr = pbs[:, :], PBS
                gi, bgi = pbx[:, :], PBX
            K = lambda n: (n, st_)

            def mm(e):
                return [e.matmul(accx, WBk[wi][:, k, hl * 128:(hl + 1) * 128], hTg[:, k, :], start=(k == 0), stop=(k == 15)) for k in range(16)]
            P.op("pe", mm, reads=wkeys(wi) + [kk for k in range(16) for kk in hTg_keys(k)], writes=[K("accx")], banks=[bx])
            P.op("dve", lambda e: e.tensor_copy(xr[:, 0:3], halo[:, h, :]), reads=["halo"], writes=[K("xrbuf")])
            P.op("act", lambda e: e.copy(xr[:, 3:515], accx), reads=[K("accx")], writes=[K("xrbuf")], banks=[bx])
            yield
            P.op("dve", lambda e: e.tensor_copy(halo[:, h, :], xr[:, 512:515]), reads=[K("xrbuf")], writes=["halo"])
            wcv = lambda j: V(WC + h * 4 + j)
            P.op("dve", lambda e: e.tensor_scalar(xc, xr[:, 0:512], wcv(0), V(BCONV + h), ALU.mult, ALU.add),
                 reads=[K("xrbuf"), "vecs"], writes=[K("xc")])
            for j in range(1, 4):
                P.op("dve", lambda e, j=j: e.scalar_tensor_tensor(xc, xr[:, j:j + 512], wcv(j), xc, ALU.mult, ALU.add),
                     reads=[K("xrbuf"), K("xc"), "vecs"], writes=[K("xc")])
            yield
            P.op("act", lambda e: e.copy(xcb, xc), reads=[K("xc")], writes=[K("xcb")])
            P.op("pe", lambda e: e.matmul(gr, wrg[:, h * 128:(h + 1) * 128], xcb, start=True, stop=True),
                 reads=["wrga", K("xcb")], writes=[K("gr")], banks=[bgr])
            P.op("pe", lambda e: e.matmul(gi, wrg[:, 1024 + h * 128:1024 + (h + 1) * 128], xcb, start=True, stop=True),
                 reads=["wrgx", K("xcb")], writes=[K("gi")], banks=[bgi])
            yield
            P.op("act", lambda e: e.activation(rr_, gr, AF.Sigmoid, bias=V(BRGA + h)), reads=[K("gr"), "vecs"], writes=[K("rr")], banks=[bgr])
            P.op("act", lambda e: e.activation(ii_, gi, AF.Sigmoid, bias=V(BRGX + h)), reads=[K("gi"), "vecs"], writes=[K("ii")], banks=[bgi])
            yield
            P.op("act", lambda e: e.activation(rr_, rr_, AF.Exp, scale=cl[:, h:h + 1]), reads=[K("rr"), "cl"], writes=[K("rr")])
            P.op("act", lambda e: e.activation(tt_, rr_, AF.Square), reads=[K("rr")], writes=[K("xrbuf")])
            yield
            P.op("act", lambda e: e.activation(tt_, tt_, AF.Ln, scale=-1.0, bias=1.0), reads=[K("xrbuf")], writes=[K("xrbuf")])
            P.op("act", lambda e: e.activation(tt_, tt_, AF.Exp, scale=0.5), reads=[K("xrbuf")], writes=[K("xrbuf")])
            P.op("dve", lambda e: e.tensor_tensor(ii_, ii_, xc, ALU.mult), reads=[K("ii"), K("xc")], writes=[K("ii")])
            yield
            P.op("dve", lambda e: e.tensor_tensor(ii_, ii_, tt_, ALU.mult), reads=[K("ii"), K("xrbuf")], writes=[K("ii")])
            P.op("dve", lambda e: e.tensor_tensor_scan(xc, rr_, ii_, carry[:, h:h + 1], ALU.mult, ALU.add),
                 reads=[K("rr"), K("ii"), "carry"], writes=[K("xc")])
            P.op("dve", lambda e: e.tensor_copy(carry[:, h:h + 1], xc[:, 511:512]), reads=[K("xc")], writes=["carry"])
            yield
            hv = xc.rearrange("p (j q c) -> p j q c", j=2, q=2)
            ho = hown[:, h, g * 256:(g + 1) * 256].rearrange("p (j c) -> p j c", j=2)
            P.op("dve", lambda e: e.tensor_scalar(ho, hv[:, :, 0, :], V(SEL), None, ALU.mult), reads=[K("xc"), "vecs"], writes=[("hown", h, g)])
            P.op("dve", lambda e: e.scalar_tensor_tensor(ho, hv[:, :, 1, :], V(SEL + 1), ho, ALU.mult, ALU.add),
                 reads=[K("xc"), "vecs", ("hown", h, g)], writes=[("hown", h, g)])
            yield

        def run_pair(ga, gb_):
            gens = [ga, gb_]
            while gens:
                for gobj in list(gens):
                    try:
                        next(gobj)
                    except StopIteration:
                        gens.remove(gobj)

        for g in range(4):
            make_hT_group(x_all, g)
            for hb in range(2):
                wi = wload_kc(w_in[:, XR0 + hb * 512:XR0 + (hb + 1) * 512])
                for hp in range(2):
                    run_pair(lru_head(g, hb * 4 + hp * 2, wi, hp * 2, 0), lru_head(g, hb * 4 + hp * 2 + 1, wi, hp * 2 + 1, 1))
        if debug:
            P.dma("sp", dbg["mods"], mods[:], reads=[("mods", 0), ("mods", 16)])
            P.dma("sp", dbg["hown"], R1[:, 0:8192], reads=[("hown", h, g) for h in range(8) for g in range(4)])

        if stop == "L1":
            P.emit_all(final_waits=list(P.pending_dma))
            return nc
        P.barrier()
        gel = r3(10240, 512)
        osq = gtbc[:, 0:2048].bitcast(BF16).rearrange("p (h t) -> p h t", h=NH)
        hT2o = R3[:, 6144:10240].bitcast(BF16).rearrange("p (k t) -> p k t", k=16)
        obuf = [hTg, hT2o]

        def obuild(g, bi):
            for i in range(4):
                slot = i % 2
                norm_tile(x_own[(g * 4 + i) * 128:(g * 4 + i + 1) * 128, :], slot, True, None)
                yield
                transpose_tile(slot, lambda k, i=i: obuf[bi][:, k, i * 128:(i + 1) * 128], 0, lambda k, i=i: [("hTo", bi, k, i)])
                yield

        def oproj(g, bi):
            hk = [("hTo", bi, k, i) for k in range(16) for i in range(4)]

            def pf(wi, hl):
                acc, key, bank = next_acc()

                def mm(e):
                    return [e.matmul(acc, WBk[wi][:, k, hl * 128:(hl + 1) * 128], obuf[bi][:, k, :], start=(k == 0), stop=(k == 15)) for k in range(16)]
                P.op("pe", mm, reads=wkeys(wi) + hk, writes=[key], banks=[bank])
                return acc, key, bank

            for qb in range(2):
                wi = wload_kc(w_in[:, qb * 512:(qb + 1) * 512])
                for hl in range(4):
                    h = qb * 4 + hl
                    acc, akey, abank = pf(wi, hl)
                    if hl % 2 == 0:
                        P.op("act", lambda e, h=h, acc=acc: e.mul(qT[:, h, g * 512:(g + 1) * 512], acc, SCALE), reads=[akey], writes=[("qT", h, g)], banks=[abank])
                    else:
                        P.op("dve", lambda e, h=h, acc=acc: e.tensor_scalar(qT[:, h, g * 512:(g + 1) * 512], acc, SCALE, None, ALU.mult), reads=[akey], writes=[("qT", h, g)], banks=[abank])
                    yield
            for gb in range(2):
                wi = wload_kc(w_in[:, 4096 + gb * 512:4096 + (gb + 1) * 512])
                for hl in range(4):
                    h = gb * 4 + hl
                    acc, akey, abank = pf(wi, hl)
                    P.op("act", lambda e, acc=acc: e.activation(gel, acc, AF.Gelu_apprx_tanh), reads=[akey], writes=["gel"], banks=[abank])
                    ho = hown[:, h, g * 512:(g + 1) * 512]
                    P.op("dve", lambda e, ho=ho: e.tensor_tensor(ho, ho, gel, ALU.mult), reads=["gel", ("hown", h, 2 * g), ("hown", h, 2 * g + 1)],
                         writes=[("olru", h, g)])
                    P.op("act", lambda e, h=h, ho=ho: e.activation(mixL[:, h, g * 512:(g + 1) * 512], ho, AF.Identity, scale=V(GLRU + h)),
                         reads=[("olru", h, g), "vecs"], writes=[("mixL", h, g)])
                    P.op("act", lambda e, h=h, ho=ho: e.activation(osq[:, h, :], ho, AF.Square), reads=[("olru", h, g)], writes=[("osq", h)])
                    yield
            for i in range(4):
                tt = g * 4 + i

                def mm(e, i=i, tt=tt):
                    return [e.matmul(pbs[:, tt:tt + 1], osq[:, h, i * 128:(i + 1) * 128], ones_bf[:, 0:1], start=(h == 0), stop=(h == 7)) for h in range(8)]
                P.op("pe", mm, reads=[("osq", h) for h in range(8)] + ["cbf"], writes=[("pbs_s", tt)], banks=[PBS])
            yield

        for _ in obuild(0, 0):
            pass
        run_pair(oproj(0, 0), obuild(1, 1))
        for _ in oproj(1, 1):
            pass
        P.op("dve", lambda e: e.tensor_scalar(rstd_l, pbs[:, 0:8], 1.0 / 1024, EPS, ALU.mult, ALU.add), reads=[("pbs_s", t) for t in range(8)], writes=["rstd_l"], banks=[PBS])
        rsqrt_inplace(rstd_l, "rstd_l")
        if stop == "L2":
            P.dma("sp", dbg["r2"], R2[:, :], reads=[("mixL", h, g) for h in range(8) for g in range(2)] + [("qT", h, g) for h in range(8) for g in range(2)])
            P.dma("sp", dbg["st"][:, 0:16], small[:, 56:72], reads=["rstd_l"])
            P.emit_all(final_waits=list(P.pending_dma))
            return nc
        P.barrier()

        acc_n["n"] = 4
        hT2 = R3[:, 6144:10240].bitcast(BF16).rearrange("p (k t) -> p k t", k=16)
        hbuf = [hTg, hT2]

        def build_gen(g, bi):
            for i in range(4):
                slot = i % 2
                norm_tile(x_all[(g * 4 + i) * 128:(g * 4 + i + 1) * 128, :], slot, True, None)
                yield
                transpose_tile(slot, lambda k, i=i: hbuf[bi][:, k, i * 128:(i + 1) * 128], 0, lambda k, i=i: [("hTb", bi, k, i)])
                yield

        def kproj_gen(g, bi):
            hk_all = [("hTb", bi, k, i) for k in range(16) for i in range(4)]
            for kb in range(2):
                wi = wload_kc(w_in[:, 1024 + kb * 512:1024 + (kb + 1) * 512])
                for hl in range(4):
                    h = kb * 4 + hl
                    acc, akey, abank = next_acc()

                    def mm(e, wi=wi, hl=hl, acc=acc):
                        return [e.matmul(acc, WBk[wi][:, k, hl * 128:(hl + 1) * 128], hbuf[bi][:, k, :], start=(k == 0), stop=(k == 15)) for k in range(16)]
                    P.op("pe", mm, reads=wkeys(wi) + hk_all, writes=[akey], banks=[abank])
                    if hl % 2 == 0:
                        P.op("act", lambda e, h=h, acc=acc: e.copy(kT[:, h, g * 512:(g + 1) * 512], acc), reads=[akey], writes=[("kT", h, g)], banks=[abank])
                    else:
                        P.op("dve", lambda e, h=h, acc=acc: e.tensor_copy(kT[:, h, g * 512:(g + 1) * 512], acc), reads=[akey], writes=[("kT", h, g)], banks=[abank])
                    yield
            for vb in range(2):
                wi = wload_kc(w_in[:, 2048 + vb * 512:2048 + (vb + 1) * 512])
                for i in range(4):
                    acc, akey, abank = next_acc()

                    def mm(e, i=i, acc=acc, wi=wi):
                        return [e.matmul(acc, hbuf[bi][:, k, i * 128:(i + 1) * 128], WBk[wi][:, k, :], start=(k == 0), stop=(k == 15)) for k in range(16)]
                    P.op("pe", mm, reads=wkeys(wi) + [("hTb", bi, k, i) for k in range(16)], writes=[akey], banks=[abank])
                    dst = Vv[:, g * 4 + i, vb * 512:(vb + 1) * 512]
                    if i % 2 == 0:
                        P.op("act", lambda e, dst=dst, acc=acc: e.copy(dst, acc), reads=[akey], writes=[("V", g * 4 + i, vb)], banks=[abank])
                    else:
                        P.op("dve", lambda e, dst=dst, acc=acc: e.tensor_copy(dst, acc), reads=[akey], writes=[("V", g * 4 + i, vb)], banks=[abank])
                    yield

        for _ in build_gen(0, 0):
            pass
        for g in range(4):
            run_pair(kproj_gen(g, g % 2), build_gen(g + 1, (g + 1) % 2) if g < 3 else iter(()))
        if debug:
            P.dma("sp", dbg["kv"], R1[:, :], reads=[("kT", h, g) for h in range(8) for g in range(4)] + [("V", a, vb) for a in range(16) for vb in range(2)])
        if stop == "K":
            P.emit_all(final_waits=list(P.pending_dma))
            return nc
        P.barrier()

        P.op("dve", lambda e: e.memset(stat_a, 0.0), writes=["stat_a"])
        SCR = 2048

        def attn_stream(h, half, sl):
            o0 = sl * SCR
            e_ = r3(o0, 512)
            acc_ = r3(o0 + 512, 512)
            hi_ = r3(o0 + 1024, 256).bitcast(FP16)
            ahi_ = r3(o0 + 1280, 256).bitcast(FP16)
            w_ = r3(o0 + 1536, 256, BF16)
            sqa = r3(o0 + 1792, 256, BF16)
            Z = big[sl][:, 0:512]
            O = big[sl][:, 512:1024]
            bz, bo = 2 * sl, 2 * sl + 1
            K = lambda n: (n, sl)
            cbase = half * 512
            P.op("dve", lambda e: e.memset(acc_, 0.0), writes=[K("acc")])
            P.op("dve", lambda e: e.memset(ahi_, 0.0), writes=[K("ahi")])
            P.op("dve", lambda e: e.memset(O, 0.0), writes=[K("O")], banks=[bo])
            yield
            glist = range(7, -1, -1) if half == 0 else range(15, -1, -1)
            for g in glist:
                j0 = g // 2
                c0 = max(j0 * 128, cbase)
                diag = (j0 * 128 >= cbase)
                mcol = 0 if g % 2 == 0 else 128
                r = slice(c0 - cbase, 512)
                d = slice(c0 - cbase, c0 - cbase + 128)
                P.op("pe", lambda e, g=g, c0=c0, r=r: e.matmul(Z[:, r], kT[:, h, g * 128:(g + 1) * 128], qT[:, h, c0:cbase + 512], start=True, stop=True),
                     reads=[], writes=[K("Z")], banks=[bz])
                P.op("act", lambda e, r=r: e.activation(e_[:, r], Z[:, r], AF.Exp), reads=[K("Z")], writes=[K("e")], banks=[bz])
                yield
                P.op("act", lambda e, r=r: e.activation(hi_[:, r], e_[:, r], AF.Ln, bias=1.0), reads=[K("e")], writes=[K("hi")])
                yield
                if diag:
                    P.op("dve", lambda e, d=d, mcol=mcol: e.tensor_tensor(hi_[:, d], hi_[:, d], masks[:, mcol:mcol + 128], ALU.mult),
                         reads=[K("hi"), "masks"], writes=[K("hi")])
                    yield

                def mm(e, r=r):
                    return [e.matmul(Z[:, r], ntri16, hi_[:, r], start=False, stop=False, skip_group_check=True),
                            e.matmul(Z[:, r], nones16, ahi_[:, r], start=False, stop=False, skip_group_check=True)]
                P.op("pe", mm, reads=[K("hi"), K("ahi"), K("Z"), K("e"), "c16"], writes=[K("Z")], banks=[bz])
                yield
                P.op("act", lambda e, r=r: e.activation(w_[:, r], Z[:, r], AF.Exp), reads=[K("Z")], writes=[K("w")], banks=[bz])
                yield
                if diag:
                    P.op("dve", lambda e, d=d, mcol=mcol: e.tensor_tensor(w_[:, d], w_[:, d], masks_bf[:, mcol:mcol + 128], ALU.mult),
                         reads=[K("w"), "masks_bf"], writes=[K("w")])
                    yield
                P.op("pe", lambda e, g=g, r=r: e.matmul(O[:, r], Vv[:, g, h * 128:(h + 1) * 128], w_[:, r], start=False, stop=False, skip_group_check=True),
                     reads=[K("w"), K("O")], writes=[K("O")], banks=[bo])
                if g > 0:
                    P.op("dve", lambda e, r=r: e.tensor_tensor(acc_[:, r], acc_[:, r], hi_[:, r], ALU.add), reads=[K("acc"), K("hi")], writes=[K("acc")])
                    yield
                    P.op("dve", lambda e, r=r: e.tensor_copy(ahi_[:, r], acc_[:, r]), reads=[K("acc")], writes=[K("ahi")])
                yield
            P.op("act", lambda e: e.activation(mixA[:, h, cbase:cbase + 512], O, AF.Identity, scale=V(GATT + h)), reads=[K("O"), "vecs"], writes=[("mixA", h, half)], banks=[bo])
            P.op("act", lambda e: e.activation(sqa, O, AF.Square), reads=[K("O")], writes=[K("sqa")], banks=[bo])
            yield

            def mm2(e):
                return [e.matmul(pbs[:, 16 + half * 4 + i:16 + half * 4 + i + 1], sqa[:, i * 128:(i + 1) * 128], ones_bf[:, 0:1], start=True, stop=True) for i in range(4)]
            P.op("pe", mm2, reads=[K("sqa"), "cbf"], writes=["pbs_stat"], banks=[PBS])
            P.op("dve", lambda e: e.tensor_tensor(stat_a[:, half * 4:half * 4 + 4], stat_a[:, half * 4:half * 4 + 4], pbs[:, 16 + half * 4:16 + half * 4 + 4], ALU.add),
                 reads=["pbs_stat", "stat_a"], writes=["stat_a"], banks=[PBS])
            yield

        def adaln_stream():
            vecs_todo = ((2, 64, None), (4, 32, GMLP), (3, 48, None), (5, 80, None))
            blocks = [(vec, cb, col, gv) for (vec, col, gv) in vecs_todo for cb in range(4)]
            nxt = adaln_wload(blocks[0][0], blocks[0][1])
            yield
            for n, (vec, cb, col, gv) in enumerate(blocks):
                wi = nxt
                if n + 1 < len(blocks):
                    nxt = adaln_wload(blocks[n + 1][0], blocks[n + 1][1])
                yield
                adaln_block(vec, cb, wi)
                if cb == 3:
                    adaln_cols(col, gv)
                yield

        def run_streams(makers, n_slots, extra=(), extra_every=10):
            pending = list(makers)
            free = list(range(n_slots))
            active = []
            extra = list(extra)
            rnd = 0
            while pending or active or extra:
                rnd += 1
                while pending and free:
                    sl = free.pop(0)
                    active.append((pending.pop(0)(sl), sl))
                for item in list(active):
                    gobj, sl = item
                    try:
                        next(gobj)
                    except StopIteration:
                        active.remove(item)
                        free.append(sl)
                if rnd % extra_every == 0 or not (pending or active):
                    for gobj in list(extra):
                        try:
                            next(gobj)
                        except StopIteration:
                            extra.remove(gobj)

        order = [(h, 1) for h in range(NH)] + [(h, 0) for h in range(NH)]
        makers = [(lambda sl, h=h, half=half: attn_stream(h, half, sl)) for (h, half) in order]
        run_streams(makers, 3, extra=[adaln_stream()])
        P.op("dve", lambda e: e.tensor_scalar(rstd_a, stat_a, 1.0 / 1024, EPS, ALU.mult, ALU.add), reads=["stat_a"], writes=["rstd_a"])
        rsqrt_inplace(rstd_a, "rstd_a")
        if debug:
            P.dma("sp", dbg["r2"], R2[:, :], reads=[("mixA", h, half) for h in range(8) for half in range(2)])
            P.dma("sp", dbg["st"][:, 0:16], small[:, 56:72], reads=["rstd_a"])
        if stop == "T":
            P.emit_all(final_waits=list(P.pending_dma))
            return nc
        bcast_from_cols(64)

        def wout_block(cb):
            wi = wload_kc(w_out[:, cb * 512:(cb + 1) * 512])
            gb = gtbc[:, cb * 512:(cb + 1) * 512].unsqueeze(1).to_broadcast([128, 16, 512])
            P.op("dve", lambda e, wi=wi, gb=gb: e.tensor_tensor(WBk[wi][:, :, :], WBk[wi][:, :, :], gb, ALU.mult), reads=wkeys(wi) + ["gtbc"], writes=wkeys(wi))
            return wi
        pre_wi = [wout_block(0), wout_block(1)]
        P.barrier()

        for tt in range(8):
            P.dma("sp", x1[:, tt, :], x_own[tt * 128:(tt + 1) * 128, :], writes=[("x1", tt, cb) for cb in range(4)])
        for cb in range(4):
            wi = pre_wi[cb] if cb < 2 else wout_block(cb)
            for tt in range(8):
                accA, kA, bA = next_acc()
                accL, kL, bL = next_acc()

                def mm(e, tt=tt, wi=wi, accA=accA, accL=accL):
                    r = [e.matmul(accA, mixA[:, h, tt * 128:(tt + 1) * 128], WBk[wi][:, h, :], start=(h == 0), stop=(h == 7)) for h in range(8)]
                    r += [e.matmul(accL, mixL[:, h, tt * 128:(tt + 1) * 128], WBk[wi][:, 8 + h, :], start=(h == 0), stop=(h == 7)) for h in range(8)]
                    return r
                P.op("pe", mm, reads=wkeys(wi), writes=[kA, kL], banks=[bA, bL])
                xs = x1[:, tt, cb * 512:(cb + 1) * 512]
                P.op("dve", lambda e, xs=xs, accA=accA, tt=tt: e.scalar_tensor_tensor(xs, accA, rstd_a[:, tt:tt + 1], xs, ALU.mult, ALU.add),
                     reads=[kA, ("x1", tt, cb)], writes=[("x1", tt, cb)], banks=[bA])
                P.op("dve", lambda e, xs=xs, accL=accL, tt=tt: e.scalar_tensor_tensor(xs, accL, rstd_l[:, tt:tt + 1], xs, ALU.mult, ALU.add),
                     reads=[kL, ("x1", tt, cb)], writes=[("x1", tt, cb)], banks=[bL])
        if debug:
            P.dma("sp", dbg["x1"], R1[:, :], reads=[("x1", tt, cb) for tt in range(8) for cb in range(4)])
        if stop == "O":
            P.emit_all(final_waits=list(P.pending_dma))
            return nc
        P.barrier()
        for tt in range(8):
            slot = tt % 2
            norm_tile(x1[:, tt, :], slot, False, [("x1", tt, cb) for cb in range(4)])
            transpose_tile(slot, lambda k, tt=tt: h2T[:, k, tt * 128:(tt + 1) * 128], 32, lambda k, tt=tt: [("h2T", k, tt)])
        bcast_from_cols(80)

        rl = [r3(6144, 512), r3(6656, 512)]
        for fg in range(16):
            w1i = wload_kc(w1[:, fg * 512:(fg + 1) * 512])
            w2i = wload_fc(w2[fg * 512:(fg + 1) * 512, :])
            gb = gtbc[:, :].unsqueeze(1).to_broadcast([128, 4, 2048])
            P.op("dve", lambda e, w2i=w2i, gb=gb: e.tensor_tensor(WBf[w2i][:, :, :], WBf[w2i][:, :, :], gb, ALU.mult), reads=wkeys(w2i) + ["gtbc"], writes=wkeys(w2i))
            for fc in range(4):
                for tg in range(2):
                    acc, akey, abank = next_acc()

                    def mm(e, fc=fc, tg=tg, acc=acc, w1i=w1i):
                        return [e.matmul(acc, WBk[w1i][:, k, fc * 128:(fc + 1) * 128], h2T[:, k, tg * 512:(tg + 1) * 512], start=(k == 0), stop=(k == 15)) for k in range(16)]
                    P.op("pe", mm, reads=wkeys(w1i) + [("h2T", k, t) for k in range(16) for t in range(tg * 4, tg * 4 + 4)], writes=[akey], banks=[abank])
                    s = (fc * 2 + tg) % 2
                    P.op("act", lambda e, s=s, acc=acc: e.activation(rl[s], acc, AF.Relu), reads=[akey], writes=[("rl", s)], banks=[abank])
                    P.op("dve", lambda e, s=s, fc=fc, tg=tg: e.tensor_tensor(hid[:, fc, tg * 512:(tg + 1) * 512], rl[s], rl[s], ALU.mult),
                         reads=[("rl", s)], writes=[("hid", fc, tg)])
            for tt in range(8):
                for cb in range(4):
                    acc, akey, abank = next_acc()

                    def mm(e, tt=tt, cb=cb, acc=acc, w2i=w2i):
                        return [e.matmul(acc, hid[:, fc, tt * 128:(tt + 1) * 128], WBf[w2i][:, fc, cb * 512:(cb + 1) * 512], start=(fc == 0), stop=(fc == 3)) for fc in range(4)]
                    P.op("pe", mm, reads=wkeys(w2i) + [("hid", fc, tt // 4) for fc in range(4)], writes=[akey], banks=[abank])
                    xs = x1[:, tt, cb * 512:(cb + 1) * 512]
                    P.op("dve", lambda e, xs=xs, acc=acc: e.tensor_tensor(xs, xs, acc, ALU.add), reads=[akey, ("x1", tt, cb)], writes=[("x1", tt, cb)], banks=[abank])

        P.dma("sp", gtbc[:], gfin_d, writes=["gtbc"])
        ob = [r3(1024, 2048), r3(3072, 2048)]
        junk = r3(5120, 1024, BF16)
        outs = []
        for tt in range(8):
            s = tt % 2
            sq = ssq[:, 2 + s:3 + s]
            xk = [("x1", tt, cb) for cb in range(4)]
            P.op("dve", lambda e, sq=sq: e.memset(sq, 0.0), writes=[("fsq", s)])
            P.op("act", lambda e, sq=sq, tt=tt: e.activation(junk, x1[:, tt, :], AF.Square, accum_out=sq), reads=xk + [("fsq", s)], writes=["junk", ("fsq", s)])
            P.op("dve", lambda e, sq=sq: e.tensor_scalar(sq, sq, 1.0 / D, EPS, ALU.mult, ALU.add), reads=[("fsq", s)], writes=[("fsq", s)])
            rsqrt_inplace(sq, ("fsq", s))
            P.op("dve", lambda e, sq=sq, tt=tt, s=s: e.scalar_tensor_tensor(ob[s], x1[:, tt, :], sq, gtbc[:], ALU.mult, ALU.mult),
                 reads=xk + [("fsq", s), "gtbc"], writes=[("ob", s)])
            outs.append(P.dma("sp", out_d[tt * 128:(tt + 1) * 128, :], ob[s], reads=[("ob", s)]))
        fw = list(outs) + [o for o in P.pending_dma if o not in outs]
        P.emit_all(final_waits=fw)
    return nc


def _core_inputs(inp, b, p):
    f = np.float32
    x = np.ascontiguousarray(inp["x"][b], dtype=f)
    xb = x.reshape(16, 128, D)
    x_own = np.ascontiguousarray(xb[p::2].reshape(1024, D))
    col = lambda v, n: np.ascontiguousarray(np.asarray(v, dtype=f).reshape(n, 128).T)
    vecs = np.zeros((128, NV), dtype=f)
    vecs[:, GMIX:GMIX + 16] = col(inp["g_norm_mix"][0], 16)
    vecs[:, GMLP:GMLP + 16] = col(inp["g_norm_mlp"][0], 16)
    vecs[:, CT:CT + 16] = col(inp["c"][b], 16)
    wc = np.asarray(inp["w_conv"][0], dtype=f)
    vecs[:, WC:WC + 32] = wc.reshape(4, 8, 128).transpose(2, 1, 0).reshape(128, 32)
    vecs[:, BCONV:BCONV + 8] = col(inp["b_conv"][0], 8)
    vecs[:, BRGA:BRGA + 8] = col(inp["b_rg_a"][0], 8)
    vecs[:, BRGX:BRGX + 8] = col(inp["b_rg_x"][0], 8)
    vecs[:, LAM:LAM + 8] = col(inp["lru_lambda"][0], 8)
    vecs[:, GATT:GATT + 8] = col(inp["g_attn_out"][0], 8)
    vecs[:, GLRU:GLRU + 8] = col(inp["g_lru_out"][0], 8)
    vecs[:, SEL] = 1.0 if p == 0 else 0.0
    vecs[:, SEL + 1] = 0.0 if p == 0 else 1.0
    s_idx = np.arange(128)[:, None]
    t_idx = np.arange(128)[None, :]
    tri_mask = (s_idx < t_idx).astype(f)
    if p == 0:
        masks = np.concatenate([tri_mask, np.zeros((128, 128), f)], axis=1)
    else:
        masks = np.concatenate([np.ones((128, 128), f), tri_mask], axis=1)
    bf = ml_dtypes.bfloat16
    cbf = np.concatenate([np.eye(128, dtype=f), (s_idx > t_idx).astype(f), np.ones((128, 128), f),
                          -(s_idx >= t_idx).astype(f), -np.ones((128, 128), f)], axis=1).astype(bf)
    return {
        "x_all": x, "x_own": x_own, "vecs": vecs,
        "b_ada": np.ascontiguousarray(np.asarray(inp["b_ada"], dtype=f).reshape(1, 6 * D)),
        "masks": np.ascontiguousarray(masks),
        "gfin": np.ascontiguousarray(np.broadcast_to(np.asarray(inp["g_norm_final"], dtype=f)[None, :], (128, D))),
        "cbf": np.ascontiguousarray(cbf),
        "w_ada": np.ascontiguousarray(inp["w_ada"][0], dtype=f),
        "w_in": np.ascontiguousarray(inp["w_in"][0], dtype=f),
        "w_rga": np.ascontiguousarray(inp["w_rg_a"][0], dtype=f),
        "w_rgx": np.ascontiguousarray(inp["w_rg_x"][0], dtype=f),
        "w_out": np.ascontiguousarray(inp["w_out"][0], dtype=f),
        "w1": np.ascontiguousarray(inp["w_mlp_in"][0], dtype=f),
        "w2": np.ascontiguousarray(inp["w_mlp_out"][0], dtype=f),
    }


_NC_CACHE = {}


def kernel(**inputs):
    inp = {k: np.asarray(v) for k, v in inputs.items()}
    if "nc" not in _NC_CACHE:
        _NC_CACHE["nc"] = build(False)
    nc = _NC_CACHE["nc"]
    in_maps = [_core_inputs(inp, c // 2, c % 2) for c in range(8)]
    res = run_bass_kernel_spmd(nc, in_maps, core_ids=list(range(8)))
    out = np.zeros((4, 16, 128, D), dtype=np.float32)
    for c in range(8):
        b, p = c // 2, c % 2
        out[b, p::2] = np.asarray(res.results[c]["out"], dtype=np.float32).reshape(8, 128, D)
    return out.reshape(4, S, D)
```
